# Optimizing a Trainium2 kernel written in Bass

```python
import math
import jax, jax.numpy as jnp
from jax import lax
import numpy as np

D_MODEL = 1024
BATCH = 16
SEQ = 2048
DEPTH = 1

ATTN_Q_HEADS = 8
ATTN_KV_HEADS = 2
ATTN_HEAD_DIM = 64
WINDOW = 128
REL_BUCKETS = 32
REL_MAX_DIST = 128
DN_HEADS = 4
DN_HEAD_DIM = 128
DN_CONV = 4
DN_CHUNK = 64
D_FF = 2816
FFN_CONV = 3
RMS_EPS = 1e-6
L2_EPS = 1e-6
N_MOD = 6
NEG_INF = -1e30

ATTN_Q_DIM = ATTN_Q_HEADS * ATTN_HEAD_DIM
ATTN_KV_DIM = ATTN_KV_HEADS * ATTN_HEAD_DIM
DN_DIM = DN_HEADS * DN_HEAD_DIM
IN_SPLIT_SIZES = (ATTN_Q_DIM, ATTN_KV_DIM, ATTN_KV_DIM, 3 * DN_DIM, DN_DIM, DN_HEADS, DN_HEADS, D_MODEL, D_MODEL)
IN_DIM = sum(IN_SPLIT_SIZES)

kernel_name = 'hybrid_swa_gdn_convffn_block'


def rms_norm(x, w):
    xf = x.astype(jnp.float32)
    y = xf * lax.rsqrt(jnp.mean(xf * xf, axis=-1, keepdims=True) + RMS_EPS)
    return (y * w.astype(jnp.float32)).astype(x.dtype)


def l2_normalize(x):
    return x * lax.rsqrt(jnp.sum(x * x, axis=-1, keepdims=True) + L2_EPS)


def causal_depthwise_conv(x, w):
    k, ch = w.shape
    return lax.conv_general_dilated(x, w[:, None, :].astype(x.dtype), window_strides=(1,), padding=[(k - 1, 0)], dimension_numbers=('NWC', 'WIO', 'NWC'), feature_group_count=ch)


def t5_causal_bucket(dist):
    dist = jnp.maximum(dist, 0)
    max_exact = REL_BUCKETS // 2
    scaled = jnp.log(jnp.maximum(dist, 1).astype(jnp.float32) / max_exact) / math.log(REL_MAX_DIST / max_exact)
    large = max_exact + (scaled * (REL_BUCKETS - max_exact)).astype(jnp.int32)
    large = jnp.minimum(large, REL_BUCKETS - 1)
    return jnp.where(dist < max_exact, dist, large)


def sliding_window_gqa(q, k, v, sinks, rel_bias):
    b, s, _, hd = q.shape
    grp = ATTN_Q_HEADS // ATTN_KV_HEADS
    nb = s // WINDOW
    qb = q.reshape(b, nb, WINDOW, ATTN_KV_HEADS, grp, hd)

    def band(t):
        tb = t.reshape(b, nb, WINDOW, ATTN_KV_HEADS, hd)
        prev = jnp.concatenate([jnp.zeros_like(tb[:, :1]), tb[:, :-1]], axis=1)
        return jnp.concatenate([prev, tb], axis=2)

    kb, vb = band(k), band(v)
    scores = jnp.einsum('bnqhgd,bnkhd->bnhgqk', qb, kb, preferred_element_type=jnp.float32) * (hd ** -0.5)
    qi = jnp.arange(WINDOW)[:, None]
    kj = jnp.arange(2 * WINDOW)[None, :]
    dist = WINDOW + qi - kj
    bias = rel_bias[t5_causal_bucket(dist)]
    bias = jnp.transpose(bias, (2, 0, 1)).reshape(ATTN_KV_HEADS, grp, WINDOW, 2 * WINDOW).astype(jnp.float32)
    in_band = (dist >= 0) & (dist < WINDOW)
    key_pos = jnp.arange(nb)[:, None, None] * WINDOW - WINDOW + kj[None]
    mask = in_band[None] & (key_pos >= 0)
    scores = jnp.where(mask[None, :, None, None], scores + bias, NEG_INF)
    sink = sinks.astype(jnp.float32).reshape(1, 1, ATTN_KV_HEADS, grp, 1, 1)
    m = jnp.maximum(jnp.max(scores, axis=-1, keepdims=True), sink)
    p = jnp.exp(scores - m)
    probs = p / (jnp.sum(p, axis=-1, keepdims=True) + jnp.exp(sink - m))
    out = jnp.einsum('bnhgqk,bnkhd->bnqhgd', probs.astype(v.dtype), vb)
    return out.reshape(b, s, ATTN_Q_DIM)


def chunk_gated_delta_rule(q, k, v, g, beta):
    b, s, nh, dk = q.shape
    dv = v.shape[-1]
    n = s // DN_CHUNK

    def chunks(t):
        return jnp.swapaxes(t, 1, 2).reshape(b, nh, n, DN_CHUNK, t.shape[-1])

    q = chunks(q) * (dk ** -0.5)
    k = chunks(k)
    v = chunks(v)
    gc = jnp.cumsum(jnp.swapaxes(g, 1, 2).reshape(b, nh, n, DN_CHUNK), axis=-1)
    bt = jnp.swapaxes(beta, 1, 2).reshape(b, nh, n, DN_CHUNK)[..., None]
    k_beta = k * bt
    v_beta = v * bt
    incl = jnp.tril(jnp.ones((DN_CHUNK, DN_CHUNK), dtype=bool))
    strict = jnp.tril(jnp.ones((DN_CHUNK, DN_CHUNK), dtype=bool), k=-1)
    diff = gc[..., :, None] - gc[..., None, :]
    decay = jnp.where(incl, jnp.exp(jnp.where(incl, diff, 0.0)), 0.0)
    eg = jnp.exp(gc)[..., None]
    lower = jnp.where(strict, jnp.einsum('bhncd,bhnmd->bhncm', k_beta, k) * decay, 0.0)
    rhs = jnp.concatenate([v_beta, k_beta * eg], axis=-1)
    eye = jnp.eye(DN_CHUNK, dtype=jnp.float32)
    sol = lax.linalg.triangular_solve(eye + lower, rhs, left_side=True, lower=True)
    u, w = sol[..., :dv], sol[..., dv:]
    intra = jnp.where(incl, jnp.einsum('bhncd,bhnmd->bhncm', q, k) * decay, 0.0)
    q_dec = q * eg
    k_tail = k * jnp.exp(gc[..., -1:] - gc)[..., None]
    g_last = jnp.exp(gc[..., -1])

    def step(state, inp):
        q_i, k_i, u_i, w_i, a_i, gl_i = inp
        v_new = u_i - jnp.einsum('bhcd,bhde->bhce', w_i, state)
        o_i = jnp.einsum('bhcd,bhde->bhce', q_i, state) + jnp.einsum('bhcm,bhme->bhce', a_i, v_new)
        state = state * gl_i[..., None, None] + jnp.einsum('bhcd,bhce->bhde', k_i, v_new)
        return state, o_i

    xs = tuple(jnp.moveaxis(t, 2, 0) for t in (q_dec, k_tail, u, w, intra, g_last))
    state0 = jnp.zeros((b, nh, dk, dv), jnp.float32)
    _, o = lax.scan(step, state0, xs)
    o = jnp.moveaxis(o, 0, 2).reshape(b, nh, s, dv)
    return jnp.swapaxes(o, 1, 2)


def gated_deltanet(qkv, beta_logits, a_logits, z, conv_w, a_log, dt_bias, norm_w):
    b, s, _ = qkv.shape
    qkv = jax.nn.silu(causal_depthwise_conv(qkv, conv_w)).astype(jnp.float32)
    q, k, v = jnp.split(qkv, 3, axis=-1)
    q = l2_normalize(q.reshape(b, s, DN_HEADS, DN_HEAD_DIM))
    k = l2_normalize(k.reshape(b, s, DN_HEADS, DN_HEAD_DIM))
    v = v.reshape(b, s, DN_HEADS, DN_HEAD_DIM)
    beta = jax.nn.sigmoid(beta_logits.astype(jnp.float32))
    g = -jnp.exp(a_log.astype(jnp.float32)) * jax.nn.softplus(a_logits.astype(jnp.float32) + dt_bias.astype(jnp.float32))
    o = chunk_gated_delta_rule(q, k, v, g, beta)
    zf = z.astype(jnp.float32).reshape(b, s, DN_HEADS, DN_HEAD_DIM)
    o = rms_norm(o, norm_w) * jax.nn.silu(zf)
    return o.reshape(b, s, DN_DIM).astype(z.dtype)


def hybrid_mixer(h, w_in, dn_conv_w, dn_a_log, dn_dt_bias, dn_norm_w, attn_sinks, rel_bias, w_attn_branch, w_dn_branch, w_out):
    b, s, _ = h.shape
    proj = h @ w_in
    splits = np.cumsum(IN_SPLIT_SIZES)[:-1].tolist()
    aq, ak, av, dqkv, dz, dbeta, da, gate_a, gate_d = jnp.split(proj, splits, axis=-1)
    y_attn = sliding_window_gqa(aq.reshape(b, s, ATTN_Q_HEADS, ATTN_HEAD_DIM), ak.reshape(b, s, ATTN_KV_HEADS, ATTN_HEAD_DIM), av.reshape(b, s, ATTN_KV_HEADS, ATTN_HEAD_DIM), attn_sinks, rel_bias)
    y_dn = gated_deltanet(dqkv, dbeta, da, dz, dn_conv_w, dn_a_log, dn_dt_bias, dn_norm_w)
    merged = jax.nn.sigmoid(gate_a) * (y_attn @ w_attn_branch) + jax.nn.sigmoid(gate_d) * (y_dn @ w_dn_branch)
    return merged @ w_out


def conv_ffn(h, w_up, conv_w, w_down):
    u = causal_depthwise_conv(h @ w_up, conv_w)
    gate, val = jnp.split(u, 2, axis=-1)
    return (jax.nn.gelu(gate, approximate=True) * val) @ w_down


def setup_inputs(seed: int = 0) -> dict:
    key = jax.random.key(seed)
    ks = jax.random.split(key, 22)
    f32 = jnp.float32
    L = DEPTH

    def normal(k, shape, scale):
        return jax.random.normal(k, shape, f32) * scale

    def gain(k, shape):
        return 1.0 + 0.05 * jax.random.normal(k, shape, f32)

    dt = jnp.exp(jax.random.uniform(ks[11], (L, DN_HEADS), f32, math.log(1e-3), math.log(1e-1)))
    return {
        'x': normal(ks[0], (BATCH, SEQ, D_MODEL), 1.0),
        'c': normal(ks[1], (BATCH, D_MODEL), 1.0),
        'ada_w': normal(ks[2], (L, D_MODEL, N_MOD * D_MODEL), 0.5 * D_MODEL ** -0.5),
        'ada_b': normal(ks[3], (L, N_MOD * D_MODEL), 0.02),
        'norm_mix_pre': gain(ks[4], (L, D_MODEL)),
        'norm_mix_post': gain(ks[5], (L, D_MODEL)),
        'norm_ffn_pre': gain(ks[6], (L, D_MODEL)),
        'norm_ffn_post': gain(ks[7], (L, D_MODEL)),
        'w_in': normal(ks[8], (L, D_MODEL, IN_DIM), D_MODEL ** -0.5),
        'dn_conv_w': normal(ks[9], (L, DN_CONV, 3 * DN_DIM), DN_CONV ** -0.5),
        'dn_a_log': jnp.log(jax.random.uniform(ks[10], (L, DN_HEADS), f32, 1.0, 16.0)),
        'dn_dt_bias': dt + jnp.log(-jnp.expm1(-dt)),
        'dn_norm_w': gain(ks[12], (L, DN_HEAD_DIM)),
        'attn_sinks': normal(ks[13], (L, ATTN_Q_HEADS), 1.0),
        'rel_bias': normal(ks[14], (REL_BUCKETS, ATTN_Q_HEADS), 0.5),
        'w_attn_branch': normal(ks[15], (L, ATTN_Q_DIM, D_MODEL), ATTN_Q_DIM ** -0.5),
        'w_dn_branch': normal(ks[16], (L, DN_DIM, D_MODEL), DN_DIM ** -0.5),
        'w_out': normal(ks[17], (L, D_MODEL, D_MODEL), D_MODEL ** -0.5),
        'ffn_w_up': normal(ks[18], (L, D_MODEL, 2 * D_FF), D_MODEL ** -0.5),
        'ffn_conv_w': normal(ks[19], (L, FFN_CONV, 2 * D_FF), FFN_CONV ** -0.5),
        'ffn_w_down': normal(ks[20], (L, D_FF, D_MODEL), D_FF ** -0.5),
    }


def reference(x, c, ada_w, ada_b, norm_mix_pre, norm_mix_post, norm_ffn_pre, norm_ffn_post, w_in, dn_conv_w, dn_a_log, dn_dt_bias, dn_norm_w, attn_sinks, rel_bias, w_attn_branch, w_dn_branch, w_out, ffn_w_up, ffn_conv_w, ffn_w_down):
    h = x
    c_act = jax.nn.silu(c)
    for l in range(DEPTH):
        mod = c_act @ ada_w[l] + ada_b[l]
        sh1, sc1, g1, sh2, sc2, g2 = [m[:, None, :] for m in jnp.split(mod, N_MOD, axis=-1)]
        u = rms_norm(h, norm_mix_pre[l]) * (1.0 + sc1) + sh1
        y = hybrid_mixer(u, w_in[l], dn_conv_w[l], dn_a_log[l], dn_dt_bias[l], dn_norm_w[l], attn_sinks[l], rel_bias, w_attn_branch[l], w_dn_branch[l], w_out[l])
        h = h + g1 * rms_norm(y, norm_mix_post[l])
        u = rms_norm(h, norm_ffn_pre[l]) * (1.0 + sc2) + sh2
        y = conv_ffn(u, ffn_w_up[l], ffn_conv_w[l], ffn_w_down[l])
        h = h + g2 * rms_norm(y, norm_ffn_post[l])
    return h
```

```python
import contextlib
import numpy as np
import concourse.bass as bass
import concourse.mybir as mybir
from concourse.bass_utils import run_bass_kernel_spmd

F32 = mybir.dt.float32
BF16 = mybir.dt.bfloat16
AF = mybir.ActivationFunctionType
ALU = mybir.AluOpType
ESZ = {F32: 4, BF16: 2}
PAGE = 512
ENGS = ("pe", "act", "dve", "pool", "sp")

D = 1024
KC = 8
SEQ = 2048
NSEQ = 2
T = 512
NB = T // 128
NT = SEQ // T
DFF = 2816
NPAIR = DFF // 128
RMS_EPS = 1e-6
L2_EPS = 1e-6
BIG = 30000.0
AQ, AK, AV, DQ, DK, DV, DZ, DBETA, DA, GA, GD = 0, 512, 640, 768, 1280, 1792, 2304, 2816, 2820, 2824, 3848
V_ADAB, V_NMP, V_NMPOST, V_NFP, V_NFPOST, V_DNC, V_FFC, V_DNW, NV = 0, 48, 56, 64, 72, 80, 128, 260, 261
C_ID, C_U, C_M1S, C_M2, C_J, NCONST = 0, 128, 256, 384, 512, 640


class Buf:
    __slots__ = ("w", "r")

    def __init__(self):
        self.w = {}
        self.r = {}


class Op:
    __slots__ = ("eng", "fn", "deps", "stream", "count", "is_dma", "needed")


def _is_ap(v):
    return hasattr(v, "tensor") and hasattr(v, "ap") and hasattr(v, "offset")


class Prog:
    def __init__(self, nc):
        self.nc = nc
        self.ops = {e: [] for e in ENGS}
        self.all_ops = []
        self.bufs = {}
        self.dma_counts = {}
        self.tracked_dram = set()
        self.out_dma = []
        self.dma_valid = {}

    def keys(self, ap):
        sp = str(ap.space)
        name = ap.tensor.name
        if sp == "PSUM":
            return [("P", name)]
        if sp == "DRAM":
            return [("D", name)] if name in self.tracked_dram else []
        esz = ESZ[ap.dtype]
        rowlen = 1
        for s in ap.tensor.shape[1:]:
            rowlen *= s
        off = ap.offset % rowlen
        dims = [(s, c) for (s, c) in list(ap.ap)[1:] if c > 1]
        pages = set()

        def rec(ds, base):
            if ds and abs(ds[0][0]) * esz >= 2 * PAGE and ds[0][1] <= 64:
                s, c = ds[0]
                for i in range(c):
                    rec(ds[1:], base + i * s)
                return
            lo = hi = base
            for s, c in ds:
                e = s * (c - 1)
                if e < 0:
                    lo += e
                else:
                    hi += e
            for pg in range((lo * esz) // PAGE, (hi * esz + esz - 1) // PAGE + 1):
                pages.add(pg)

        rec(dims, off)
        return [("S", name, pg) for pg in pages]

    def _buf(self, k):
        b = self.bufs.get(k)
        if b is None:
            b = self.bufs[k] = Buf()
        return b

    def _record(self, op, reads, writes):
        pe = op.eng == "pe" and not op.is_dma
        deps = {}
        rb = [self._buf(k) for ap in reads if str(ap.space) != "PSUM" for k in self.keys(ap)]
        wb = [self._buf(k) for ap in writes for k in self.keys(ap)]
        wb += [self._buf(k) for ap in reads if str(ap.space) == "PSUM" for k in self.keys(ap)]
        for b in rb:
            for s, c in b.w.items():
                if deps.get(s, 0) < c:
                    deps[s] = c
        for b in wb:
            for s, c in b.w.items():
                if pe and s == "pe":
                    continue
                if deps.get(s, 0) < c:
                    deps[s] = c
            for s, c in b.r.items():
                if deps.get(s, 0) < c:
                    deps[s] = c
        op.deps = deps
        self.ops[op.eng].append(op)
        self.all_ops.append(op)
        return rb, wb

    def _commit(self, op, rb, wb):
        s, c = op.stream, op.count
        for b in rb:
            if b.r.get(s, 0) < c:
                b.r[s] = c
        for b in wb:
            b.w = {s: c}
            b.r = {}

    def I(self, eng, meth, **kw):
        reads, writes = [], []
        for k, v in kw.items():
            if _is_ap(v):
                (writes if k in ("out", "accum_out", "ap") else reads).append(v)
        op = Op()
        op.eng = eng
        op.is_dma = False
        op.needed = False
        op.fn = lambda e, meth=meth, kw=kw: getattr(e, meth)(**kw)
        rb, wb = self._record(op, reads, writes)
        op.stream = eng
        op.count = len(self.ops[eng])
        self._commit(op, rb, wb)
        return op

    def dma(self, eng, out, in_, key, is_output=False, after=(), last=True, **kw):
        op = Op()
        op.eng = eng
        op.is_dma = True
        op.needed = False
        op.fn = lambda e, out=out, in_=in_, kw=kw: e.dma_start(out=out, in_=in_, **kw)
        rb, wb = self._record(op, [in_], [out])
        for o_ in after:
            if op.deps.get(o_.stream, 0) < o_.count:
                op.deps[o_.stream] = o_.count
        op.deps.pop("dma:" + key, None)
        c = self.dma_counts.get(key, 0) + 16
        self.dma_counts[key] = c
        op.stream = "dma:" + key
        op.count = c
        if last:
            self.dma_valid.setdefault(op.stream, []).append(c)
        self._commit(op, rb, wb)
        if is_output:
            self.out_dma.append((op.stream, c))
        return op

    def emit(self):
        nc = self.nc
        for o in self.all_ops:
            for s, c in o.deps.items():
                if not s.startswith("dma:"):
                    self.ops[s][c - 1].needed = True
        final = {}
        for e in ENGS:
            n = 0
            for i, o in enumerate(self.ops[e]):
                if not o.is_dma and o.needed:
                    n += 1
                    final[(e, i + 1)] = n
        with contextlib.ExitStack() as stack:
            sems = {}
            for e in ENGS:
                sems[e] = stack.enter_context(nc.semaphore("s_" + e))
            for k in self.dma_counts:
                sems["dma:" + k] = stack.enter_context(nc.semaphore("d_" + k))
            block = stack.enter_context(nc.Block())
            engmap = {"pe": block.tensor, "act": block.scalar, "dve": block.vector,
                      "pool": block.gpsimd, "sp": block.sync}

            def make(e):
                def body(eng):
                    waited = {}
                    for o in self.ops[e]:
                        for s, c in o.deps.items():
                            if s.startswith("dma:"):
                                v = next(x for x in self.dma_valid[s] if x >= c)
                            else:
                                v = final[(s, c)]
                            if waited.get(s, 0) < v:
                                eng.wait_ge(sems[s], v)
                                waited[s] = v
                        ins = o.fn(eng)
                        if o.is_dma:
                            ins.then_inc(sems[o.stream], 16)
                        elif o.needed:
                            ins.then_inc(sems[e], 1)
                    if e == "sp":
                        for s, c in self.out_dma:
                            if waited.get(s, 0) < c:
                                eng.wait_ge(sems[s], c)
                                waited[s] = c
                return body

            for e in ENGS:
                engmap[e](make(e))


def _t5_bucket(dist):
    dist = np.maximum(dist, 0)
    max_exact = 16
    scaled = np.log(np.maximum(dist, 1).astype(np.float32) / np.float32(max_exact)) / np.float32(np.log(128 / 16))
    large = max_exact + (scaled.astype(np.float32) * np.float32(16)).astype(np.int32)
    large = np.minimum(large, 31)
    return np.where(dist < max_exact, dist, large)


def _host_consts():
    c = np.zeros((128, NCONST), np.float32)
    i = np.arange(128)
    c[:, C_ID:C_ID + 128] = np.eye(128, dtype=np.float32)
    c[:, C_U:C_U + 128] = (i[:, None] <= i[None, :]).astype(np.float32)
    c[:, C_M1S:C_M1S + 128] = np.where(i[None, :] >= i[:, None], BIG, 0.0)
    c[:, C_M2:C_M2 + 128] = np.where(i[None, :] < i[:, None], -BIG, 0.0)
    c[:, C_J:C_J + 128] = (i[:, None] + i[None, :] == 127).astype(np.float32)
    sel = np.zeros((33, 512), np.float32)
    for jp in range(255):
        if jp >= 128:
            sel[_t5_bucket(np.array(255 - jp)), jp] = 1.0
        else:
            sel[32, jp] = 1.0
        if jp <= 127:
            sel[_t5_bucket(np.array(127 - jp)), 256 + jp] = 1.0
        else:
            sel[32, 256 + jp] = 1.0
    lm = np.zeros((128, 7 * 128), np.float32)
    for l in range(7):
        n = 1 << l
        m = ((i[:, None] // (2 * n)) == (i[None, :] // (2 * n))) & ((i[:, None] % (2 * n)) >= n) & ((i[None, :] % (2 * n)) < n)
        lm[:, l * 128:(l + 1) * 128] = m
    return c, sel, lm


class _Stop(Exception):
    pass


def build_program(ntiles=NSEQ * NT, dbg=(), stop=None):
    nc = bass.Bass("TRN2", target_bir_lowering=False)
    p = Prog(nc)

    def dram(name, shape, dt=F32, kind="ExternalInput"):
        return nc.dram_tensor(name, list(shape), dt, kind=kind).ap()

    x = dram("x", [NSEQ, SEQ, D])
    cT = dram("cT", [D, NSEQ])
    ada_w = dram("ada_w", [D, 6 * D])
    vecs = dram("vecs", [128, NV])
    rowc = dram("rowc", [1, 16])
    w_in = dram("w_in", [D, 4872])
    w_a = dram("w_a", [512, D])
    w_d = dram("w_d", [512, D])
    w_o = dram("w_o", [D, D])
    w_up = dram("w_up", [D, 2 * DFF])
    w_dn = dram("w_dn", [DFF, D])
    relb = dram("relb", [32, 8])
    consts = dram("consts", [128, NCONST])
    selc = dram("selc", [33, 512])
    lmask = dram("lmask", [128, 7 * 128])
    out = dram("out", [NSEQ, SEQ, D], kind="ExternalOutput")
    escr = dram("escr", [2, 8, 256], kind="Internal")
    p.tracked_dram.add("escr")
    dbg_outs = {}

    with contextlib.ExitStack() as st:
        ARENA_E = 104448
        arena = st.enter_context(nc.sbuf_tensor("arena", [128, ARENA_E], BF16))
        psums = [st.enter_context(nc.psum_tensor(f"ps{i}", [128, 512], F32)) for i in range(8)]
        state = {"off": 0, "ps": 0}

        def alloc(shape, dt=F32):
            n = 1
            for s in shape:
                n *= s
            nbytes = n * ESZ[dt]
            off = state["off"]
            al = PAGE if nbytes >= PAGE else 64
            off = (off + al - 1) // al * al
            state["off"] = off + nbytes
            assert state["off"] <= ARENA_E * 2, ("SBUF arena overflow", state["off"])
            v = arena[:, off // 2: (off + nbytes) // 2]
            if dt == F32:
                v = v.bitcast(F32)
            if len(shape) == 2:
                v = v.rearrange("p (a b) -> p a b", a=shape[0], b=shape[1])
            elif len(shape) == 3:
                v = v.rearrange("p (a b c) -> p a b c", a=shape[0], b=shape[1], c=shape[2])
            return v

        class Ring:
            def __init__(self, shape, dt, n):
                self.t = [alloc(shape, dt) for _ in range(n)]
                self.i = 0

            def next(self):
                v = self.t[self.i % len(self.t)]
                self.i += 1
                return v

        NROT = 6
        psA = [psums[6], psums[7]]

        def psum():
            t = psums[state["ps"] % NROT]
            state["ps"] += 1
            return t

        def mm(out, lhsT, rhs, start=True, stop=True):
            p.I("pe", "matmul", out=out, lhsT=lhsT, rhs=rhs, start=start, stop=stop)

        def tr(out, in_, ident):
            p.I("pe", "transpose", out=out, in_=in_, identity=ident)

        def act(out, in_, func, bias=None, scale=None, accum_out=None):
            kw = dict(out=out, in_=in_, func=func)
            if bias is not None:
                kw["bias"] = bias
            if scale is not None:
                kw["scale"] = scale
            if accum_out is not None:
                kw["accum_out"] = accum_out
            p.I("act", "activation", **kw)

        def tt(eng, out, in0, in1, op):
            p.I(eng, "tensor_tensor", out=out, in0=in0, in1=in1, op=op)

        def ts(eng, out, in0, s1, op0, s2=None, op1=None):
            kw = dict(out=out, in0=in0, scalar1=s1, scalar2=s2, op0=op0)
            if op1 is not None:
                kw["op1"] = op1
            p.I(eng, "tensor_scalar", **kw)

        def stt(out, in0, scalar, in1, op0, op1):
            p.I("dve", "scalar_tensor_tensor", out=out, in0=in0, scalar=scalar, in1=in1, op0=op0, op1=op1)

        def cp(eng, out, in_):
            if eng == "act":
                act(out, in_, AF.Copy)
            else:
                p.I(eng, "tensor_copy", out=out, in_=in_)

        def memset(eng, ap, val):
            p.I(eng, "memset", ap=ap, constant=val)

        def dbg_dump(name, ap, shape):
            if name not in dbg:
                return
            cnt = dbg_outs.setdefault(name, [])
            idx = len(cnt)
            d = nc.dram_tensor(f"dbg_{name}_{idx}", list(shape), ap.dtype, kind="ExternalOutput").ap()
            cnt.append(f"dbg_{name}_{idx}")
            p.dma("sp", d, ap, key=f"dbg{len(p.dma_counts)}", is_output=True)

        cst = alloc([NCONST])
        ident = cst[:, C_ID:C_ID + 128]
        Umat = cst[:, C_U:C_U + 128]
        M1S = cst[:, C_M1S:C_M1S + 128]
        M2 = cst[:, C_M2:C_M2 + 128]
        Jm = cst[:, C_J:C_J + 128]
        identb = alloc([128], BF16)
        mk = alloc([7, 128], BF16)
        onesb = alloc([128], BF16)
        vec = alloc([NV])
        rowb = alloc([16])
        negexpalog = alloc([4])
        expsink = alloc([8])
        modT = None
        cact = alloc([KC, NSEQ])
        biasT = [alloc([8, 128]), alloc([8, 128])]
        g1w = alloc([D])
        g2w = alloc([D])
        w1s = alloc([KC])
        w2s = alloc([KC])
        halo_dn = [alloc([128])[:, 0:36].rearrange("p (c j) -> p c j", j=3) for _ in range(2)]
        halo_ffn = [alloc([128])[:, 0:88].rearrange("p (c j) -> p c j", j=2) for _ in range(2)]
        Sst = alloc([4, 128])
        Sbf = [alloc([4, 128], BF16), alloc([4, 128], BF16)]
        kT = alloc([128 + T], BF16)
        vaug = alloc([NB + 1, 2, 66], BF16)
        xh = [alloc([NB, D]), alloc([NB, D])]
        uT = alloc([KC, T], BF16)
        NSLOT = 4
        wring = [alloc([4096], BF16) for _ in range(NSLOT)]
        small = Ring([64], F32, 12)

        units = []

        unit_len = {}
        unit_srcs = {}

        def add_unit(name, srcs, n=4096):
            unit_len[name] = n
            unit_srcs[name] = srcs
            u = dram("wsc_" + name, [128, 4096], BF16, kind="Internal")
            p.tracked_dram.add("wsc_" + name)
            units.append((name, u, srcs))
            return u

        def v3(u, a, b):
            return u[:, 0:a * b].rearrange("p (a b) -> p a b", a=a, b=b)

        def wv(w, c0, n):
            return w[:, c0:c0 + n].rearrange("(kc p) n -> p kc n", p=128)

        add_unit("q", [(lambda u, g=g, c=c: u[:, :].rearrange("p (kc c g h) -> p kc c g h", kc=8, c=4, g=2, h=64)[:, :, c, g, :],
                        wv(w_in, AQ + g * 256 + c * 64, 64)) for g in range(2) for c in range(4)])
        add_unit("kvg", [(lambda u: v3(u, 8, 264)[:, :, 0:256], wv(w_in, AK, 256)),
                         (lambda u: v3(u, 8, 264)[:, :, 256:264], wv(w_in, DBETA, 8))], n=8 * 264)
        for nm, c0 in (("dq", DQ), ("dk", DK), ("dv", DV), ("z", DZ), ("ga0", GA), ("ga1", GA + 512), ("gd0", GD), ("gd1", GD + 512)):
            add_unit(nm, [(lambda u: v3(u, 8, 512), wv(w_in, c0, 512))])
        for hf in range(2):
            add_unit(f"wa{hf}", [(lambda u: v3(u, 4, 512), w_a[:, hf * 512:(hf + 1) * 512].rearrange("(kc p) n -> p kc n", p=128))], n=2048)
            add_unit(f"wd{hf}", [(lambda u: v3(u, 4, 512), w_d[:, hf * 512:(hf + 1) * 512].rearrange("(kc p) n -> p kc n", p=128))], n=2048)
        add_unit("wo0", [(lambda u: v3(u, 8, 512), wv(w_o, 0, 512))])
        add_unit("wo1", [(lambda u: v3(u, 8, 512), wv(w_o, 512, 512))])
        for j in range(11):
            add_unit(f"up{j}", [(lambda u, two=two: u[:, :].rearrange("p (kc two n) -> p kc two n", kc=8, two=2, n=256)[:, :, two, :],
                                 wv(w_up, two * DFF + 256 * j, 256)) for two in range(2)])
        for j in range(6):
            nfc = min(4, NPAIR - 4 * j)
            add_unit(f"dn{j}", [(lambda u, nfc=nfc: v3(u, nfc, 1024),
                                 w_dn[4 * j * 128:(4 * j + nfc) * 128, :].rearrange("(fc p) n -> p fc n", p=128))], n=nfc * 1024)
        unit_ap = {name: u for name, u, _ in units}
        def convert(names, after=()):
            for name, u, srcs in units:
                if name in names:
                    for f, s_ in srcs:
                        p.dma("pool", f(u), s_, key="cv_" + name, after=after)

        def last_pe():
            return [p.ops["pe"][-1]] if p.ops["pe"] else []

        converted = set()
        LA = 7

        def load_unit(name):
            def ld(slot, key):
                n = unit_len[name]
                p.dma("sp", slot[:, 0:n], unit_ap[name][:, 0:n], key=key)
            return ld

        def load_unit_direct(name):
            def ld(slot, key):
                srcs_ = unit_srcs[name]
                for i_, (f, s_) in enumerate(srcs_):
                    p.dma("pool", f(slot), s_, key=key + "s", last=(i_ == len(srcs_) - 1))
                n = unit_len[name]
                p.dma("sp", unit_ap[name][:, 0:n], slot[:, 0:n], key="cv_" + name)
            return ld

        def load_ada(v, part):
            def ld(slot, key):
                c0 = v * 1024 + part * 512
                p.dma("pool", slot[:, :].rearrange("p (kc n) -> p kc n", kc=KC, n=512),
                      ada_w[:, c0:c0 + 512].rearrange("(kc p) n -> p kc n", p=128), key=key + "s")
            return ld

        class WStream:
            def __init__(self, seq):
                self.seq = seq
                self.issued = 0
                self.cur = 0

            def acquire(self, name):
                assert self.seq[self.cur][0] == name, (self.seq[self.cur][0], name)
                while self.issued < min(len(self.seq), self.cur + NSLOT - 1):
                    self.seq[self.issued][1](wring[self.issued % NSLOT], f"wr{self.issued % NSLOT}")
                    self.issued += 1
                slot = wring[self.cur % NSLOT]
                self.cur += 1
                return slot

        def ada_units(v):
            return [(f"ada{v}_{part}", load_ada(v, part)) for part in range(2)]

        base = ["q", "kvg", "dq", "dk", "dv", "z", "wa0", "ga0", "wa1", "ga1", "wd0", "gd0", "wd1", "gd1"]
        tail_o = ["wo0", "wo1"]
        ups = [f"up{j}" for j in range(11)]
        dns = [f"dn{j}" for j in range(6)]
        U_ = lambda names: [(n, load_unit(n)) for n in names]
        UD_ = lambda names: [(n, load_unit_direct(n)) for n in names]
        seq0 = ada_units(0) + ada_units(1) + UD_(base) + ada_units(2) + UD_(tail_o) + ada_units(3) + ada_units(4) + UD_(ups) + ada_units(5) + UD_(dns)
        seqn = U_(base + tail_o + ups + dns)
        W = WStream(seq0 + [e for _ in range(ntiles - 1) for e in seqn])

        p.dma("sp", cst, consts, key="c0")
        p.dma("sp", vec, vecs, key="c1")
        p.dma("sp", rowb, bass.AP(rowc.tensor, 0, [[0, 128], [1, 16]]), key="c2")
        p.dma("sp", cact, cT.rearrange("(kc p) s -> p kc s", p=128), key="c7")
        p.dma("sp", xh[0], x[0, 0:T, :].rearrange("(b p) d -> p b d", p=128), key="x0")
        cp("dve", identb, ident)
        memset("dve", onesb, 1.0)
        memset("dve", vaug[:, :, :, 64:66], 1.0)
        act(cact, cact, AF.Silu)
        act(negexpalog, rowb[:, 0:4], AF.Exp)
        ts("dve", negexpalog, negexpalog, -1.0, ALU.mult)
        act(expsink, rowb[:, 8:16], AF.Exp)

        modv = [alloc([128]) for _ in range(6)]
        r0 = alloc([4096])

        def modcol(v):
            return modv[v][:, 0:KC * NSEQ].rearrange("p (j s) -> p j s", s=NSEQ)

        cactb = alloc([KC, NSEQ], BF16)
        cp("dve", cactb, cact)

        def ada_vec(v):
            pT = psum()
            rows = r0[:, 0:512]
            for part in range(2):
                sl = v3(W.acquire(f"ada{v}_{part}"), KC, 512)
                pr = psum()
                for kc in range(KC):
                    mm(pr[0:NSEQ, :], cactb[:, kc, :], sl[:, kc, :], start=(kc == 0), stop=(kc == KC - 1))
                cp("dve", rows[0:NSEQ, :], pr[0:NSEQ, :])
                for j4 in range(4):
                    j = part * 4 + j4
                    tr(pT[:, NSEQ * j:NSEQ * (j + 1)], rows[0:NSEQ, j4 * 128:(j4 + 1) * 128], ident[0:NSEQ, 0:NSEQ])
            tt("dve", modcol(v), pT[:, 0:KC * NSEQ].rearrange("p (j s) -> p j s", s=NSEQ),
               vec[:, V_ADAB + 8 * v:V_ADAB + 8 * v + 8].unsqueeze(2).to_broadcast([128, KC, NSEQ]), ALU.add)

        ada_vec(0)
        ada_vec(1)

        ov_save = state["off"]
        raug = alloc([8])
        selsb = alloc([512])
        Esb = alloc([512])
        Hk = alloc([8, 128])
        lm_st = alloc([7 * 128])
        p.dma("sp", lm_st, lmask, key="c8")
        cp("dve", mk, lm_st[:, :].rearrange("p (l j) -> p l j", l=7))
        memset("dve", raug[32:33, :], -BIG)
        p.dma("sp", raug[0:32, :], relb, key="c3")
        p.dma("sp", selsb[0:33, :], selc, key="c4")
        pE = psum()
        mm(pE[0:8, :], raug[0:33, 0:8], selsb[0:33, :])
        cp("dve", Esb[0:8, :], pE[0:8, :])
        p.dma("sp", escr.rearrange("t h j -> h t j"), Esb[0:8, :].rearrange("h (t j) -> h t j", t=2), key="c5")
        for t_ in range(2):
            p.dma("sp", Hk, bass.AP(escr.tensor, t_ * 8 * 256, [[1, 128], [256, 8], [1, 128]]), key="c6")
            for hh in range(2):
                pb_ = psum()
                for h in range(4):
                    mm(pb_[:, h * 128:(h + 1) * 128], Hk[:, hh * 4 + h, :], Jm)
                cp("dve", biasT[t_][:, hh * 4:(hh + 1) * 4, :], pb_[:, :].rearrange("p (h q) -> p h q", h=4))
        state["off"] = ov_save

        OV = state["off"]
        xn = [r0[:, b * 1024:(b + 1) * 1024] for b in range(NB)]
        DK_ = 2
        dF = [[r0[:, (2 * j + i) * 512:(2 * j + i + 1) * 512].rearrange("p (h i) -> p h i", h=4) for i in range(2)] for j in range(DK_)]
        m1 = r0[:, 2048:4096].bitcast(BF16).rearrange("p (c t) -> p c t", c=KC)
        junkF = r0[:, 0:256].bitcast(BF16)
        OV1 = state["off"]
        qT = alloc([4, T], BF16)
        yaT = alloc([4, T], BF16)
        dqk = alloc([KC, T], BF16)
        dqT = dqk[:, 0:4, :]
        dkT = dqk[:, 4:8, :]
        dvT = alloc([4, T], BF16)
        szT = alloc([4, T], BF16)
        ydT = alloc([4, T], BF16)
        mT = dqk
        gcol = alloc([NB, 4])
        bcol = alloc([NB, 4])
        gtmp = alloc([NB, 4])
        OV2 = state["off"]
        ends = []
        BK_ = 9
        Bset = [dict(acc=alloc([T]), sl=alloc([T]), sqb=alloc([T], BF16)) for _ in range(BK_)]
        ends.append(state["off"]); state["off"] = OV2
        sc_r = Ring([512], F32, 2)
        pT_r = Ring([4, 128], BF16, 4)
        yat_r = Ring([512], BF16, 2)
        sg_r = Ring([512], F32, 2)
        DN_NAMES = ["ktok", "vtok", "qd", "AqkT", "A", "Ua", "Ub", "Va", "Vb", "Yp", "Al0", "Al1", "TbT", "kw", "ktl", "nwT"]
        Dset = []
        for j in range(DK_):
            d_ = {n: alloc([4, 128], BF16) for n in DN_NAMES}
            d_["sm"] = alloc([64])
            d_["F0"], d_["F1"] = dF[j]
            Dset.append(d_)
        ends.append(state["off"]); state["off"] = OV2
        ytmp_r = Ring([D], F32, 2)
        ends.append(state["off"])
        OV_MIX_END = max(ends)
        state["off"] = OV1
        HK_ = 4
        Hset = [dict(a0=alloc([T]), a1=alloc([T]), tmp=alloc([T])) for _ in range(HK_)]
        hidT = alloc([NPAIR, T], BF16)
        ytmp2_r = Ring([D], F32, 2)
        state["off"] = max(state["off"], OV_MIX_END)
        print("SBUF arena bytes used:", state["off"], "of", ARENA_E * 2)

        def VC(base, c):
            return vec[:, base + c:base + c + 1]

        def run_tasks(tasks, admit_n=None):
            pending = list(tasks)
            active = []
            counters = {}
            admit_n = admit_n or {}
            while pending or active:
                admitted = {}
                rest = []
                for (cls, k, fn) in pending:
                    nact = sum(1 for a in active if a[0] == cls)
                    if admitted.get(cls, 0) < admit_n.get(cls, 1) and nact < k and admitted.get(cls, 0) >= 0:
                        c = counters.get(cls, 0)
                        counters[cls] = c + 1
                        active.append((cls, fn(c % k)))
                        admitted[cls] = admitted.get(cls, 0) + 1
                    else:
                        admitted[cls] = -1
                        rest.append((cls, k, fn))
                pending = rest
                still = []
                for cls, g in active:
                    try:
                        next(g)
                        still.append((cls, g))
                    except StopIteration:
                        pass
                active = still

        def rstd_from_ss(ss_ap, n, inv_n):
            r = small.next()[:, 0:n]
            ts("dve", r, ss_ap, inv_n, ALU.mult, RMS_EPS, ALU.add)
            act(r, r, AF.Sqrt)
            p.I("dve", "reciprocal", out=r, in_=r)
            return r

        def prenorm_part1(xt):
            ss = small.next()[:, 0:NB]
            for b in range(NB):
                act(xn[b][:, 0:512].bitcast(BF16), xt[:, b, :], AF.Square, accum_out=ss[:, b:b + 1])
            r = rstd_from_ss(ss, NB, 1.0 / D)
            for b in range(NB):
                if b % 2 == 0:
                    act(xn[b], xt[:, b, :], AF.Copy, scale=r[:, b:b + 1])
                else:
                    ts("dve", xn[b], xt[:, b, :], r[:, b:b + 1], ALU.mult)

        def prenorm_part2(wcol, shv, s):
            for kc in range(KC):
                ps = psum()
                for b in range(NB):
                    tr(ps[:, b * 128:(b + 1) * 128], xn[b][:, kc * 128:(kc + 1) * 128], ident)
                if kc % 2 == 0:
                    act(uT[:, kc, :], ps[:, :], AF.Identity, bias=modcol(shv)[:, kc, s:s + 1], scale=wcol[:, kc:kc + 1])
                else:
                    ts("dve", uT[:, kc, :], ps[:, :], wcol[:, kc:kc + 1], ALU.mult, modcol(shv)[:, kc, s:s + 1], ALU.add)

        def prenorm_to_uT(xt, wcol, shv, s):
            prenorm_part1(xt)
            prenorm_part2(wcol, shv, s)

        def postnorm_residual(xt, b, phs, gw, ytr, jk=None):
            jk = junkF if jk is None else jk
            ss2 = small.next()[:, 0:2]
            for hf in range(2):
                act(jk, phs[hf][:, :], AF.Square, accum_out=ss2[:, hf:hf + 1])
            ss = small.next()[:, 0:1]
            tt("dve", ss, ss2[:, 0:1], ss2[:, 1:2], ALU.add)
            r = rstd_from_ss(ss, 1, 1.0 / D)
            yt = ytr.next()
            for hf in range(2):
                stt(yt[:, hf * 512:(hf + 1) * 512], phs[hf][:, :], r[:, 0:1], gw[:, hf * 512:(hf + 1) * 512], ALU.mult, ALU.mult)
            tt("pool", xt[:, b, :], xt[:, b, :], yt, ALU.add)

        def postnorm_batch(xt, bankpairs, gw, ytr, jk):
            ss2 = small.next()[:, 0:2 * NB]
            for b in range(NB):
                for hf in range(2):
                    act(jk, bankpairs[b][hf][:, :], AF.Square, accum_out=ss2[:, 2 * b + hf:2 * b + hf + 1])
            ss2v = ss2.rearrange("p (b h) -> p b h", h=2)
            ss = small.next()[:, 0:NB]
            tt("dve", ss, ss2v[:, :, 0], ss2v[:, :, 1], ALU.add)
            r = rstd_from_ss(ss, NB, 1.0 / D)
            for b in range(NB):
                yt = ytr.next()
                for hf in range(2):
                    stt(yt[:, hf * 512:(hf + 1) * 512], bankpairs[b][hf][:, :], r[:, b:b + 1], gw[:, hf * 512:(hf + 1) * 512], ALU.mult, ALU.mult)
                tt("pool", xt[:, b, :], xt[:, b, :], yt, ALU.add)

        def setup_wcol(wdst, scv, vbase, s):
            stt(wdst, modcol(scv)[:, :, s], 1.0, vec[:, vbase:vbase + 8], ALU.add, ALU.mult)

        def setup_gw(gw, gv, vbase, s):
            gc_ = small.next()[:, 0:8]
            tt("dve", gc_, modcol(gv)[:, :, s], vec[:, vbase:vbase + 8], ALU.mult)
            for hf in range(2):
                ps = psum()
                for k4 in range(4):
                    kc = hf * 4 + k4
                    cb = xn[k4][:, 0:128]
                    cp("dve", cb, gc_[:, kc:kc + 1].to_broadcast([128, 128]))
                    mm(ps[:, k4 * 128:(k4 + 1) * 128], cb, ident)
                cp("act", gw[:, hf * 512:(hf + 1) * 512], ps[:, :])

        def stage(name):
            if stop == name:
                raise _Stop()

        tile_no = 0
        hoisted = {"done": False}
        try:
          stage('P')
          for s in range(NSEQ):
            if tile_no >= ntiles:
                break
            memset("pool", Sst, 0.0)
            memset("pool", Sbf[0], 0.0)
            memset("pool", halo_dn[tile_no % 2], 0.0)
            memset("pool", halo_ffn[tile_no % 2], 0.0)
            sbc = {"i": 0}
            for ti in range(NT):
                if tile_no >= ntiles:
                    break
                first = (ti == 0)
                xt = xh[tile_no % 2]
                nt_ = tile_no + 1
                if nt_ < ntiles:
                    ns, nti = divmod(nt_, NT)
                    p.dma("sp", xh[nt_ % 2], x[ns, nti * T:(nti + 1) * T, :].rearrange("(b p) d -> p b d", p=128), key=f"x{nt_ % 2}")
                if first:
                    setup_wcol(w1s, 1, V_NMP, s)
                if not hoisted["done"]:
                    prenorm_part1(xt)
                hoisted["done"] = False
                prenorm_part2(w1s, 0, s)
                if "uT" in dbg and tile_no == 0:
                    dbg_dump("uT", uT, [128, KC, T])
                stage('A')
                sl = W.acquire("q")
                wq = v3(sl, 8, 512)
                for c in range(4):
                    ps = psum()
                    for kc in range(KC):
                        mm(ps[:, :], wq[:, kc, c * 128:(c + 1) * 128], uT[:, kc, :], start=(kc == 0), stop=(kc == KC - 1))
                    act(qT[:, c, :], ps[:, :], AF.Copy, scale=0.125)
                sl = W.acquire("kvg")
                wk = v3(sl, 8, 264)
                ps = psum()
                for kc in range(KC):
                    mm(ps[:, :], wk[:, kc, 0:128], uT[:, kc, :], start=(kc == 0), stop=(kc == KC - 1))
                cp("act", kT[:, 128:128 + T], ps[:, :])
                ps = psum()
                for b in range(NB):
                    for kc in range(KC):
                        mm(ps[:, b * 128:(b + 1) * 128], uT[:, kc, b * 128:(b + 1) * 128], wk[:, kc, 128:256], start=(kc == 0), stop=(kc == KC - 1))
                cp("dve", vaug[:, 1:NB + 1, :, 0:64], ps[:, :].rearrange("p (b k d) -> p b k d", b=NB, k=2))
                ps = psum()
                for b in range(NB):
                    for kc in range(KC):
                        mm(ps[:, b * 8:(b + 1) * 8], uT[:, kc, b * 128:(b + 1) * 128], wk[:, kc, 256:264], start=(kc == 0), stop=(kc == KC - 1))
                pg3 = ps[:, 0:NB * 8].rearrange("p (b c) -> p b c", c=8)
                act(bcol, pg3[:, :, 0:4], AF.Sigmoid)
                tt("dve", gtmp, pg3[:, :, 4:8], rowb[:, 4:8].unsqueeze(1).to_broadcast([128, NB, 4]), ALU.add)
                act(gtmp, gtmp, AF.Exp)
                act(gtmp, gtmp, AF.Ln, bias=1.0)
                tt("dve", gcol, gtmp, negexpalog[:, 0:4].unsqueeze(1).to_broadcast([128, NB, 4]), ALU.mult)
                wunit = {}

                def conv_task(ui, hh, dst):
                    def gen(j):
                        B_ = Bset[j]
                        ch = ui * 4 + hh
                        if hh == 0:
                            wunit[ui] = v3(W.acquire(("dq", "dk", "dv")[ui]), 8, 512)
                        wu = wunit[ui]
                        ps = psum()
                        for kc in range(KC):
                            mm(ps[:, :], wu[:, kc, hh * 128:(hh + 1) * 128], uT[:, kc, :], start=(kc == 0), stop=(kc == KC - 1))
                        acc = B_["acc"]
                        hold = halo_dn[tile_no % 2][:, ch, :]
                        act(acc, ps[:, :], AF.Copy, scale=VC(V_DNC, 3 * 12 + ch))
                        cp("act", halo_dn[(tile_no + 1) % 2][:, ch, :], ps[:, T - 3:T])
                        for jt in range(3):
                            sh = 3 - jt
                            wj = VC(V_DNC, jt * 12 + ch)
                            stt(acc[:, sh:T], ps[:, 0:T - sh], wj, acc[:, sh:T], ALU.mult, ALU.add)
                            stt(acc[:, 0:sh], hold[:, 3 - sh:3], wj, acc[:, 0:sh], ALU.mult, ALU.add)
                        yield
                        yield
                        if ui == 2:
                            act(dst[:, hh, :], acc, AF.Silu)
                            return
                        slv, sqb = B_["sl"], B_["sqb"]
                        act(slv, acc, AF.Silu)
                        tt("pool", sqb, slv, slv, ALU.mult)
                        yield
                        ps2 = psum()
                        mm(ps2[:, :], onesb, sqb)
                        act(acc, ps2[:, :], AF.Ln, bias=L2_EPS)
                        yield
                        act(acc, acc, AF.Exp, scale=-0.5, bias=(-0.5 * float(np.log(128.0)) if ui == 0 else 0.0))
                        tt("dve", dst[:, hh, :], slv, acc, ALU.mult)
                    return gen

                def attn_task(j):
                    for b in range(NB):
                        gb = ti * NB + b
                        kbs = ([0] if gb > 0 else []) + [1]
                        po = [psA[0], psA[1]]
                        for kv in range(2):
                            pts = {}
                            for kb in kbs:
                                ps = psum()
                                kcol = (b + kb) * 128
                                mm(ps[:, :], kT[64 * kv:64 * kv + 64, kcol:kcol + 128], qT[64 * kv:64 * kv + 64, :, b * 128:(b + 1) * 128])
                                sc = sc_r.next()
                                tt("dve", sc, ps[:, :], biasT[kb][:, 4 * kv:4 * kv + 4, :], ALU.add)
                                pt = pT_r.next()
                                act(pt, sc, AF.Exp)
                                pts[kb] = pt
                            for g in range(4):
                                for n_, kb in enumerate(kbs):
                                    mm(po[kv][:, g * 65:(g + 1) * 65], pts[kb][:, g, :], vaug[:, b + kb, kv, 0:65],
                                       start=(n_ == 0), stop=(n_ == len(kbs) - 1))
                            yield
                        yat = yat_r.next()
                        for kv in range(2):
                            pov = po[kv][:, 0:260].rearrange("p (g d) -> p g d", d=65)
                            den = small.next()[:, 0:4]
                            tt("dve", den, pov[:, :, 64], expsink[:, 4 * kv:4 * kv + 4], ALU.add)
                            p.I("dve", "reciprocal", out=den, in_=den)
                            tt("dve", yat[:, kv * 256:(kv + 1) * 256].rearrange("p (g d) -> p g d", d=64), pov[:, :, 0:64],
                               den.unsqueeze(2).to_broadcast([128, 4, 64]), ALU.mult)
                        yield
                        psb = psum()[:, :].bitcast(BF16)
                        for c in range(4):
                            tr(psb[:, c * 128:(c + 1) * 128], yat[:, c * 128:(c + 1) * 128], identb)
                        cp("act", yaT[:, :, b * 128:(b + 1) * 128], psb[:, 0:512].rearrange("p (c t) -> p c t", c=4))
                        yield
                    cp("pool", kT[:, 0:128], kT[:, T:T + 128])
                    cp("pool", vaug[:, 0, :, 0:64], vaug[:, NB, :, 0:64])

                def dn_task(b):
                    def gen(j):
                        S_ = Dset[j]
                        blk = slice(b * 128, (b + 1) * 128)
                        F0, F1 = S_["F0"], S_["F1"]
                        ktok, vtok = S_["ktok"], S_["vtok"]
                        sm = S_["sm"]
                        gc_c, ngc_c, egc, tail, glb, nbeta = [sm[:, 4 * i:4 * i + 4] for i in range(6)]
                        r4 = lambda t_: t_[:, :].rearrange("p (h i) -> p h i", h=4)
                        for (srcT, dstk, eng) in ((dkT, ktok, "act"), (dvT, vtok, "dve")):
                            psb = psum()[:, :].bitcast(BF16)
                            for h in range(4):
                                tr(psb[:, h * 128:(h + 1) * 128], srcT[:, h, blk], identb)
                            cp(eng, dstk, psb[:, 0:512].rearrange("p (h d) -> p h d", h=4))
                        pgc = psum()
                        mm(pgc[:, 0:4], Umat, gcol[:, b, :])
                        cp("dve", gc_c, pgc[:, 0:4])
                        ts("dve", ngc_c, gc_c, -1.0, ALU.mult)
                        cp("dve", F0, gcol[:, b, :].unsqueeze(2).to_broadcast([128, 4, 128]))
                        ts("dve", nbeta, bcol[:, b, :], -1.0, ALU.mult)
                        act(egc, gc_c, AF.Exp)
                        yield
                        pgcb = psum()
                        for h in range(4):
                            mm(pgcb[:, h * 128:(h + 1) * 128], F0[:, h, :], Umat)
                        pgcb3 = r4(pgcb)
                        act(F1, pgcb3, AF.Exp)
                        tt("dve", tail, pgcb3[:, :, 127], gc_c, ALU.subtract)
                        act(glb, pgcb3[:, :, 127], AF.Exp)
                        tt("dve", F0, pgcb3, M1S.unsqueeze(1).to_broadcast([128, 4, 128]), ALU.add)
                        tt("dve", S_["qd"], dqT[:, :, blk], F1, ALU.mult)
                        tt("dve", F1, pgcb3, M2.unsqueeze(1).to_broadcast([128, 4, 128]), ALU.add)
                        act(tail, tail, AF.Exp)
                        yield
                        for h in range(4):
                            act(F0[:, h, :], F0[:, h, :], AF.Exp, scale=-1.0, bias=gc_c[:, h:h + 1])
                            act(F1[:, h, :], F1[:, h, :], AF.Exp, scale=1.0, bias=ngc_c[:, h:h + 1])
                        pkk = psum()
                        for h in range(4):
                            mm(pkk[:, h * 128:(h + 1) * 128], dkT[:, h, blk], dkT[:, h, blk])
                        pqk = psum()
                        for h in range(4):
                            mm(pqk[:, h * 128:(h + 1) * 128], dkT[:, h, blk], dqT[:, h, blk])
                        A = S_["A"]
                        for h in range(4):
                            stt(A[:, h, :], pkk[:, h * 128:(h + 1) * 128], nbeta[:, h:h + 1], F0[:, h, :], ALU.mult, ALU.mult)
                        tt("dve", S_["AqkT"], r4(pqk), F1, ALU.mult)
                        tt("pool", S_["kw"], ktok, egc.unsqueeze(2).to_broadcast([128, 4, 128]), ALU.mult)
                        tt("pool", S_["ktl"], ktok, tail.unsqueeze(2).to_broadcast([128, 4, 128]), ALU.mult)
                        yield
                        mkb = lambda l: mk[:, l, :].unsqueeze(1).to_broadcast([128, 4, 128])
                        Al = [S_["Al0"], S_["Al1"]]
                        tt("pool", Al[0], A, mkb(0), ALU.mult)
                        yield
                        pat = psum()[:, :].bitcast(BF16)
                        for h in range(4):
                            tr(pat[:, h * 128:(h + 1) * 128], Al[0][:, h, :], identb)
                        pat3 = pat[:, 0:512].rearrange("p (h i) -> p h i", h=4)
                        Ub_ = [S_["Ua"], S_["Ub"]]
                        Vb_ = [S_["Va"], S_["Vb"]]
                        U, V = Ub_[0], Vb_[0]
                        tt("dve", V, pat3, identb.unsqueeze(1).to_broadcast([128, 4, 128]), ALU.add)
                        tt("pool", U, Al[0], identb.unsqueeze(1).to_broadcast([128, 4, 128]), ALU.add)
                        tt("pool", Al[1], A, mkb(1), ALU.mult)
                        yield
                        Yp = S_["Yp"]
                        for lvl in range(1, 7):
                            Acur = Al[lvl % 2]
                            pY = psum()
                            for h in range(4):
                                mm(pY[:, h * 128:(h + 1) * 128], Acur[:, h, :], V[:, h, :])
                            cp("act", Yp, r4(pY))
                            if lvl < 6:
                                tt("pool", Al[(lvl + 1) % 2], A, mkb(lvl + 1), ALU.mult)
                            yield
                            pu = psum()
                            for h in range(4):
                                mm(pu[:, h * 128:(h + 1) * 128], Yp[:, h, :], U[:, h, :])
                            pv = psum()
                            for h in range(4):
                                mm(pv[:, h * 128:(h + 1) * 128], U[:, h, :], Yp[:, h, :])
                            Un, Vn = Ub_[lvl % 2], Vb_[lvl % 2]
                            tt("dve", Un, r4(pu), U, ALU.add)
                            tt("dve", Vn, r4(pv), V, ALU.add)
                            U, V = Un, Vn
                            yield
                        X = V
                        TbT = S_["TbT"]
                        tt("dve", TbT, X, bcol[:, b, :].unsqueeze(2).to_broadcast([128, 4, 128]), ALU.mult)
                        yield
                        pw = psum()
                        for h in range(4):
                            mm(pw[:, h * 128:(h + 1) * 128], S_["kw"][:, h, :], TbT[:, h, :])
                        nwT = S_["nwT"]
                        act(nwT, r4(pw), AF.Copy, scale=-1.0)
                        yield
                        Sb = Sbf[sbc["i"] % 2]
                        pv = psum()
                        for h in range(4):
                            mm(pv[:, h * 128:(h + 1) * 128], TbT[:, h, :], vtok[:, h, :], start=True, stop=False)
                            mm(pv[:, h * 128:(h + 1) * 128], nwT[:, h, :], Sb[:, h, :], start=False, stop=True)
                        vnew = S_["kw"]
                        cp("act", vnew, pv[:, :].rearrange("p (h e) -> p h e", h=4))
                        yield
                        po_ = psum()
                        for h in range(4):
                            mm(po_[:, h * 128:(h + 1) * 128], Sb[:, h, :], S_["qd"][:, h, :], start=True, stop=False)
                            mm(po_[:, h * 128:(h + 1) * 128], vnew[:, h, :], S_["AqkT"][:, h, :], start=False, stop=True)
                        psu = psum()
                        for h in range(4):
                            mm(psu[:, h * 128:(h + 1) * 128], S_["ktl"][:, h, :], vnew[:, h, :])
                        for h in range(4):
                            stt(Sst[:, h, :], Sst[:, h, :], glb[:, h:h + 1], psu[:, h * 128:(h + 1) * 128], ALU.mult, ALU.add)
                        sbc["i"] += 1
                        cp("act", Sbf[sbc["i"] % 2], Sst)
                        sq = S_["A"]
                        act(sq, r4(po_), AF.Square)
                        cp("dve", F1, r4(po_))
                        yield
                        pss = psum()
                        mm(pss[:, :], onesb, sq[:, :, :].rearrange("p h i -> p (h i)"))
                        act(F0, r4(pss), AF.Ln, scale=1.0 / 128.0, bias=RMS_EPS)
                        yield
                        act(F0, F0, AF.Exp, scale=-0.5)
                        if "dn_o" in dbg and tile_no == 0 and b <= 1:
                            dbg_dump("dn_o", F1, [128, 4, 128])
                        tt("dve", F1, F1, F0, ALU.mult)
                        stt(ydT[:, :, blk], F1, VC(V_DNW, 0), szT[:, :, blk], ALU.mult, ALU.mult)
                    return gen

                def mergeA_task(j):
                    for hf in range(2):
                        wb_ = v3(W.acquire(f"wa{hf}"), 4, 512)
                        wg = v3(W.acquire(f"ga{hf}"), 8, 512)
                        for cc in range(4):
                            c = hf * 4 + cc
                            pg_ = psum()
                            for kc in range(KC):
                                mm(pg_[:, :], wg[:, kc, cc * 128:(cc + 1) * 128], uT[:, kc, :], start=(kc == 0), stop=(kc == KC - 1))
                            sg = sg_r.next()
                            act(sg, pg_[:, :], AF.Sigmoid)
                            yield
                            pb_ = psum()
                            for kc in range(4):
                                mm(pb_[:, :], wb_[:, kc, cc * 128:(cc + 1) * 128], yaT[:, kc, :], start=(kc == 0), stop=(kc == 3))
                            tt("dve", m1[:, c, :], pb_[:, :], sg, ALU.mult)
                            yield

                run_tasks([("B", BK_, conv_task(ui, hh, dst)) for ui, dst in enumerate((dqT, dkT, dvT)) for hh in range(4)], admit_n={"B": 2})
                sl = W.acquire("z")
                wu = v3(sl, 8, 512)
                for hh in range(4):
                    ps = psum()
                    for kc in range(KC):
                        mm(ps[:, :], wu[:, kc, hh * 128:(hh + 1) * 128], uT[:, kc, :], start=(kc == 0), stop=(kc == KC - 1))
                    act(szT[:, hh, :], ps[:, :], AF.Silu)
                stage('B')
                def attn_merge_task(j):
                    yield from attn_task(j)
                    yield
                    yield from mergeA_task(j)

                run_tasks([("D", DK_, dn_task(b)) for b in range(NB)] + [("C", 1, attn_merge_task)])
                if "yaT" in dbg and tile_no <= 1:
                    dbg_dump("yaT", yaT, [128, 4, T])
                if "ydT" in dbg and tile_no <= 1:
                    dbg_dump("ydT", ydT, [128, 4, T])
                stage('D')
                for hf in range(2):
                    wb_ = v3(W.acquire(f"wd{hf}"), 4, 512)
                    wg = v3(W.acquire(f"gd{hf}"), 8, 512)
                    for cc in range(4):
                        c = hf * 4 + cc
                        pb_ = psum()
                        for kc in range(4):
                            mm(pb_[:, :], wb_[:, kc, cc * 128:(cc + 1) * 128], ydT[:, kc, :], start=(kc == 0), stop=(kc == 3))
                        pg_ = psum()
                        for kc in range(KC):
                            mm(pg_[:, :], wg[:, kc, cc * 128:(cc + 1) * 128], uT[:, kc, :], start=(kc == 0), stop=(kc == KC - 1))
                        sg = sg_r.next()
                        act(sg, pg_[:, :], AF.Sigmoid)
                        tt("dve", sg, pb_[:, :], sg, ALU.mult)
                        tt("pool", mT[:, c, :], sg, m1[:, c, :], ALU.add)
                if "mT" in dbg and tile_no == 0:
                    dbg_dump("mT", mT, [128, KC, T])
                stage('E')
                if first:
                    if tile_no == 0:
                        ada_vec(2)
                    setup_gw(g1w, 2, V_NMPOST, s)
                wo = [v3(W.acquire("wo0"), 8, 512), v3(W.acquire("wo1"), 8, 512)]
                pend = None
                for b in range(NB):
                    phs = [psum(), psum()]
                    for hf in range(2):
                        for kc in range(KC):
                            mm(phs[hf][:, :], mT[:, kc, b * 128:(b + 1) * 128], wo[hf][:, kc, :], start=(kc == 0), stop=(kc == KC - 1))
                    if pend is not None:
                        postnorm_residual(xt, pend[0], pend[1], g1w, ytmp_r)
                    pend = (b, phs)
                postnorm_residual(xt, pend[0], pend[1], g1w, ytmp_r)
                if "h1" in dbg and tile_no == 0:
                    dbg_dump("h1", xt, [128, NB, D])
                stage('F')
                if first:
                    if tile_no == 0:
                        ada_vec(3)
                        ada_vec(4)
                    setup_wcol(w2s, 4, V_NFP, s)
                prenorm_to_uT(xt, w2s, 3, s)
                stage('G')
                wunit = {}

                def ffn_task(pair):
                    def gen(j):
                        H_ = Hset[j]
                        jj, pp_ = divmod(pair, 2)
                        if pp_ == 0:
                            wunit[jj] = W.acquire(f"up{jj}")[:, :].rearrange("p (kc two n) -> p kc two n", kc=8, two=2, n=256)
                        wu = wunit[jj]
                        for two in range(2):
                            ps = psum()
                            for kc in range(KC):
                                mm(ps[:, :], wu[:, kc, two, pp_ * 128:(pp_ + 1) * 128], uT[:, kc, :], start=(kc == 0), stop=(kc == KC - 1))
                            ch = two * NPAIR + pair
                            acc = H_[f"a{two}"]
                            hold = halo_ffn[tile_no % 2][:, ch, :]
                            wc = lambda jt: VC(V_FFC, jt * 2 * NPAIR + ch)
                            act(acc, ps[:, :], AF.Copy, scale=wc(2))
                            cp("act", halo_ffn[(tile_no + 1) % 2][:, ch, :], ps[:, T - 2:T])
                            if two == 1:
                                tmp = H_["tmp"]
                                act(tmp[:, 1:T], ps[:, 0:T - 1], AF.Copy, scale=wc(1))
                                act(tmp[:, 0:1], hold[:, 1:2], AF.Copy, scale=wc(1))
                            for jt in range(2 if two == 0 else 1):
                                sh = 2 - jt
                                stt(acc[:, sh:T], ps[:, 0:T - sh], wc(jt), acc[:, sh:T], ALU.mult, ALU.add)
                                stt(acc[:, 0:sh], hold[:, 2 - sh:2], wc(jt), acc[:, 0:sh], ALU.mult, ALU.add)
                            if two == 1:
                                tt("pool", acc, acc, tmp, ALU.add)
                        yield
                        yield
                        act(H_["a0"], H_["a0"], AF.Gelu_apprx_tanh)
                        tt("pool", hidT[:, pair, :], H_["a0"], H_["a1"], ALU.mult)
                    return gen

                run_tasks([("H", HK_, ffn_task(pair)) for pair in range(NPAIR)])
                stage('H')
                if first:
                    if tile_no == 0:
                        ada_vec(5)
                    setup_gw(g2w, 5, V_NFPOST, s)
                banks = [psum() for _ in range(NROT)] + psA
                for j in range(6):
                    nfc = min(4, NPAIR - 4 * j)
                    sl = W.acquire(f"dn{j}")
                    wdn_ = v3(sl, nfc, 1024)
                    for fcl in range(nfc):
                        fc = 4 * j + fcl
                        for b in range(NB):
                            for hf in range(2):
                                mm(banks[b * 2 + hf][:, :], hidT[:, fc, b * 128:(b + 1) * 128], wdn_[:, fcl, hf * 512:(hf + 1) * 512],
                                   start=(fc == 0), stop=(fc == NPAIR - 1))
                if nt_ < ntiles:
                    prenorm_part1(xh[nt_ % 2])
                    hoisted["done"] = True
                jk2 = Hset[0]["tmp"][:, 0:256].bitcast(BF16)
                postnorm_batch(xt, [[banks[b * 2], banks[b * 2 + 1]] for b in range(NB)], g2w, ytmp2_r, jk2)
                p.dma("sp", out[s, ti * T:(ti + 1) * T, :].rearrange("(b p) d -> p b d", p=128), xt, key=f"o{tile_no % 2}", is_output=True)
                tile_no += 1
        except _Stop:
            pass
        p.emit()
    return nc, dbg_outs, p


_CACHE = {}


def _layout_inputs(inputs, core):
    f = lambda a: np.ascontiguousarray(np.asarray(a, dtype=np.float32))
    b0 = core * NSEQ
    vecs = np.zeros((128, NV), np.float32)
    col = lambda v: np.asarray(v, np.float32).reshape(-1, 128).T
    vecs[:, V_ADAB:V_ADAB + 48] = col(inputs["ada_b"][0])
    vecs[:, V_NMP:V_NMP + 8] = col(inputs["norm_mix_pre"][0])
    vecs[:, V_NMPOST:V_NMPOST + 8] = col(inputs["norm_mix_post"][0])
    vecs[:, V_NFP:V_NFP + 8] = col(inputs["norm_ffn_pre"][0])
    vecs[:, V_NFPOST:V_NFPOST + 8] = col(inputs["norm_ffn_post"][0])
    dnc = np.asarray(inputs["dn_conv_w"][0], np.float32)
    for j in range(4):
        vecs[:, V_DNC + j * 12:V_DNC + (j + 1) * 12] = col(dnc[j])
    ffc = np.asarray(inputs["ffn_conv_w"][0], np.float32)
    for j in range(3):
        vecs[:, V_FFC + j * 44:V_FFC + (j + 1) * 44] = col(ffc[j])
    vecs[:, V_DNW] = np.asarray(inputs["dn_norm_w"][0], np.float32)
    rowc = np.concatenate([np.asarray(inputs["dn_a_log"][0], np.float32), np.asarray(inputs["dn_dt_bias"][0], np.float32),
                           np.asarray(inputs["attn_sinks"][0], np.float32)])[None, :]
    return {
        "x": f(inputs["x"][b0:b0 + NSEQ]),
        "cT": f(np.asarray(inputs["c"][b0:b0 + NSEQ]).T),
        "vecs": vecs,
        "rowc": f(rowc),
    }


def kernel(**inputs):
    n = 8
    if "nc" not in _CACHE:
        _CACHE["nc"] = build_program()[0]
        _CACHE["consts"] = _host_consts()
    nc = _CACHE["nc"]
    consts, selc, lmask = _CACHE["consts"]
    f = lambda a: np.ascontiguousarray(np.asarray(a, dtype=np.float32))
    shared = {
        "ada_w": f(inputs["ada_w"][0]), "w_in": f(inputs["w_in"][0]), "w_a": f(inputs["w_attn_branch"][0]),
        "w_d": f(inputs["w_dn_branch"][0]), "w_o": f(inputs["w_out"][0]), "w_up": f(inputs["ffn_w_up"][0]),
        "w_dn": f(inputs["ffn_w_down"][0]), "relb": f(inputs["rel_bias"]), "consts": consts, "selc": selc, "lmask": lmask,
    }
    in_maps = []
    for c in range(n):
        m = dict(shared)
        m.update(_layout_inputs(inputs, c))
        in_maps.append(m)
    res = run_bass_kernel_spmd(nc, in_maps, core_ids=list(range(n)))
    return np.concatenate([np.asarray(r["out"]) for r in res.results], axis=0).astype(np.float32)
```

```python
import contextlib
import numpy as np
import concourse.bass as bass
import concourse.mybir as mybir
from concourse.bass_utils import run_bass_kernel_spmd

F32 = mybir.dt.float32
BF16 = mybir.dt.bfloat16
AF = mybir.ActivationFunctionType
ALU = mybir.AluOpType
ESZ = {F32: 4, BF16: 2}
PAGE = 512
ENGS = ("pe", "act", "dve", "pool", "sp")

D = 1024
KC = 8
SEQ = 2048
NSEQ = 2
T = 512
NB = T // 128
NT = SEQ // T
DFF = 2816
NPAIR = DFF // 128
RMS_EPS = 1e-6
L2_EPS = 1e-6
BIG = 30000.0
AQ, AK, AV, DQ, DK, DV, DZ, DBETA, DA, GA, GD = 0, 512, 640, 768, 1280, 1792, 2304, 2816, 2820, 2824, 3848
V_ADAB, V_NMP, V_NMPOST, V_NFP, V_NFPOST, V_DNC, V_FFC, V_DNW, NV = 0, 48, 56, 64, 72, 80, 128, 260, 261
C_ID, C_U, C_M1S, C_M2, C_J, NCONST = 0, 128, 256, 384, 512, 640


class Buf:
    __slots__ = ("w", "r")

    def __init__(self):
        self.w = {}
        self.r = {}


class Op:
    __slots__ = ("eng", "fn", "deps", "stream", "count", "is_dma", "needed")


def _is_ap(v):
    return hasattr(v, "tensor") and hasattr(v, "ap") and hasattr(v, "offset")


class Prog:
    def __init__(self, nc):
        self.nc = nc
        self.ops = {e: [] for e in ENGS}
        self.all_ops = []
        self.bufs = {}
        self.dma_counts = {}
        self.tracked_dram = set()
        self.out_dma = []
        self.dma_valid = {}

    def keys(self, ap):
        sp = str(ap.space)
        name = ap.tensor.name
        if sp == "PSUM":
            return [("P", name)]
        if sp == "DRAM":
            return [("D", name)] if name in self.tracked_dram else []
        esz = ESZ[ap.dtype]
        rowlen = 1
        for s in ap.tensor.shape[1:]:
            rowlen *= s
        off = ap.offset % rowlen
        dims = [(s, c) for (s, c) in list(ap.ap)[1:] if c > 1]
        pages = set()

        def rec(ds, base):
            if ds and abs(ds[0][0]) * esz >= 2 * PAGE and ds[0][1] <= 64:
                s, c = ds[0]
                for i in range(c):
                    rec(ds[1:], base + i * s)
                return
            lo = hi = base
            for s, c in ds:
                e = s * (c - 1)
                if e < 0:
                    lo += e
                else:
                    hi += e
            for pg in range((lo * esz) // PAGE, (hi * esz + esz - 1) // PAGE + 1):
                pages.add(pg)

        rec(dims, off)
        return [("S", name, pg) for pg in pages]

    def _buf(self, k):
        b = self.bufs.get(k)
        if b is None:
            b = self.bufs[k] = Buf()
        return b

    def _record(self, op, reads, writes):
        pe = op.eng == "pe" and not op.is_dma
        deps = {}
        rb = [self._buf(k) for ap in reads if str(ap.space) != "PSUM" for k in self.keys(ap)]
        wb = [self._buf(k) for ap in writes for k in self.keys(ap)]
        wb += [self._buf(k) for ap in reads if str(ap.space) == "PSUM" for k in self.keys(ap)]
        for b in rb:
            for s, c in b.w.items():
                if deps.get(s, 0) < c:
                    deps[s] = c
        for b in wb:
            for s, c in b.w.items():
                if pe and s == "pe":
                    continue
                if deps.get(s, 0) < c:
                    deps[s] = c
            for s, c in b.r.items():
                if deps.get(s, 0) < c:
                    deps[s] = c
        op.deps = deps
        self.ops[op.eng].append(op)
        self.all_ops.append(op)
        return rb, wb

    def _commit(self, op, rb, wb):
        s, c = op.stream, op.count
        for b in rb:
            if b.r.get(s, 0) < c:
                b.r[s] = c
        for b in wb:
            b.w = {s: c}
            b.r = {}

    def I(self, eng, meth, **kw):
        reads, writes = [], []
        for k, v in kw.items():
            if _is_ap(v):
                (writes if k in ("out", "accum_out", "ap") else reads).append(v)
        op = Op()
        op.eng = eng
        op.is_dma = False
        op.needed = False
        op.fn = lambda e, meth=meth, kw=kw: getattr(e, meth)(**kw)
        rb, wb = self._record(op, reads, writes)
        op.stream = eng
        op.count = len(self.ops[eng])
        self._commit(op, rb, wb)
        return op

    def dma(self, eng, out, in_, key, is_output=False, after=(), last=True, **kw):
        op = Op()
        op.eng = eng
        op.is_dma = True
        op.needed = False
        op.fn = lambda e, out=out, in_=in_, kw=kw: e.dma_start(out=out, in_=in_, **kw)
        rb, wb = self._record(op, [in_], [out])
        for o_ in after:
            if op.deps.get(o_.stream, 0) < o_.count:
                op.deps[o_.stream] = o_.count
        op.deps.pop("dma:" + key, None)
        c = self.dma_counts.get(key, 0) + 16
        self.dma_counts[key] = c
        op.stream = "dma:" + key
        op.count = c
        if last:
            self.dma_valid.setdefault(op.stream, []).append(c)
        self._commit(op, rb, wb)
        if is_output:
            self.out_dma.append((op.stream, c))
        return op

    def emit(self):
        nc = self.nc
        for o in self.all_ops:
            for s, c in o.deps.items():
                if not s.startswith("dma:"):
                    self.ops[s][c - 1].needed = True
        final = {}
        for e in ENGS:
            n = 0
            for i, o in enumerate(self.ops[e]):
                if not o.is_dma and o.needed:
                    n += 1
                    final[(e, i + 1)] = n
        with contextlib.ExitStack() as stack:
            sems = {}
            for e in ENGS:
                sems[e] = stack.enter_context(nc.semaphore("s_" + e))
            for k in self.dma_counts:
                sems["dma:" + k] = stack.enter_context(nc.semaphore("d_" + k))
            block = stack.enter_context(nc.Block())
            engmap = {"pe": block.tensor, "act": block.scalar, "dve": block.vector,
                      "pool": block.gpsimd, "sp": block.sync}

            def make(e):
                def body(eng):
                    waited = {}
                    for o in self.ops[e]:
                        for s, c in o.deps.items():
                            if s.startswith("dma:"):
                                v = next(x for x in self.dma_valid[s] if x >= c)
                            else:
                                v = final[(s, c)]
                            if waited.get(s, 0) < v:
                                eng.wait_ge(sems[s], v)
                                waited[s] = v
                        ins = o.fn(eng)
                        if o.is_dma:
                            ins.then_inc(sems[o.stream], 16)
                        elif o.needed:
                            ins.then_inc(sems[e], 1)
                    if e == "sp":
                        for s, c in self.out_dma:
                            if waited.get(s, 0) < c:
                                eng.wait_ge(sems[s], c)
                                waited[s] = c
                return body

            for e in ENGS:
                engmap[e](make(e))


def _t5_bucket(dist):
    dist = np.maximum(dist, 0)
    max_exact = 16
    scaled = np.log(np.maximum(dist, 1).astype(np.float32) / np.float32(max_exact)) / np.float32(np.log(128 / 16))
    large = max_exact + (scaled.astype(np.float32) * np.float32(16)).astype(np.int32)
    large = np.minimum(large, 31)
    return np.where(dist < max_exact, dist, large)


def _host_consts():
    c = np.zeros((128, NCONST), np.float32)
    i = np.arange(128)
    c[:, C_ID:C_ID + 128] = np.eye(128, dtype=np.float32)
    c[:, C_U:C_U + 128] = (i[:, None] <= i[None, :]).astype(np.float32)
    c[:, C_M1S:C_M1S + 128] = np.where(i[None, :] >= i[:, None], BIG, 0.0)
    c[:, C_M2:C_M2 + 128] = np.where(i[None, :] < i[:, None], -BIG, 0.0)
    c[:, C_J:C_J + 128] = (i[:, None] + i[None, :] == 127).astype(np.float32)
    sel = np.zeros((33, 512), np.float32)
    for jp in range(255):
        if jp >= 128:
            sel[_t5_bucket(np.array(255 - jp)), jp] = 1.0
        else:
            sel[32, jp] = 1.0
        if jp <= 127:
            sel[_t5_bucket(np.array(127 - jp)), 256 + jp] = 1.0
        else:
            sel[32, 256 + jp] = 1.0
    lm = np.zeros((128, 7 * 128), np.float32)
    for l in range(7):
        n = 1 << l
        m = ((i[:, None] // (2 * n)) == (i[None, :] // (2 * n))) & ((i[:, None] % (2 * n)) >= n) & ((i[None, :] % (2 * n)) < n)
        lm[:, l * 128:(l + 1) * 128] = m
    return c, sel, lm


class _Stop(Exception):
    pass


def build_program(ntiles=NSEQ * NT, dbg=(), stop=None):
    nc = bass.Bass("TRN2", target_bir_lowering=False)
    p = Prog(nc)

    def dram(name, shape, dt=F32, kind="ExternalInput"):
        return nc.dram_tensor(name, list(shape), dt, kind=kind).ap()

    x = dram("x", [NSEQ, SEQ, D])
    cT = dram("cT", [D, NSEQ])
    ada_w = dram("ada_w", [D, 6 * D])
    vecs = dram("vecs", [128, NV])
    rowc = dram("rowc", [1, 16])
    w_in = dram("w_in", [D, 4872])
    w_a = dram("w_a", [512, D])
    w_d = dram("w_d", [512, D])
    w_o = dram("w_o", [D, D])
    w_up = dram("w_up", [D, 2 * DFF])
    w_dn = dram("w_dn", [DFF, D])
    relb = dram("relb", [32, 8])
    consts = dram("consts", [128, NCONST])
    selc = dram("selc", [33, 512])
    lmask = dram("lmask", [128, 7 * 128])
    out = dram("out", [NSEQ, SEQ, D], kind="ExternalOutput")
    escr = dram("escr", [2, 8, 256], kind="Internal")
    p.tracked_dram.add("escr")
    dbg_outs = {}

    with contextlib.ExitStack() as st:
        ARENA_E = 104448
        arena = st.enter_context(nc.sbuf_tensor("arena", [128, ARENA_E], BF16))
        psums = [st.enter_context(nc.psum_tensor(f"ps{i}", [128, 512], F32)) for i in range(8)]
        state = {"off": 0, "ps": 0}

        def alloc(shape, dt=F32):
            n = 1
            for s in shape:
                n *= s
            nbytes = n * ESZ[dt]
            off = state["off"]
            al = PAGE if nbytes >= PAGE else 64
            off = (off + al - 1) // al * al
            state["off"] = off + nbytes
            assert state["off"] <= ARENA_E * 2, ("SBUF arena overflow", state["off"])
            v = arena[:, off // 2: (off + nbytes) // 2]
            if dt == F32:
                v = v.bitcast(F32)
            if len(shape) == 2:
                v = v.rearrange("p (a b) -> p a b", a=shape[0], b=shape[1])
            elif len(shape) == 3:
                v = v.rearrange("p (a b c) -> p a b c", a=shape[0], b=shape[1], c=shape[2])
            return v

        class Ring:
            def __init__(self, shape, dt, n):
                self.t = [alloc(shape, dt) for _ in range(n)]
                self.i = 0

            def next(self):
                v = self.t[self.i % len(self.t)]
                self.i += 1
                return v

        NROT = 6
        psA = [psums[6], psums[7]]

        state["nrot"] = NROT

        def psum():
            t = psums[state["ps"] % state["nrot"]]
            state["ps"] += 1
            return t

        def mm(out, lhsT, rhs, start=True, stop=True):
            p.I("pe", "matmul", out=out, lhsT=lhsT, rhs=rhs, start=start, stop=stop)

        def tr(out, in_, ident):
            p.I("pe", "transpose", out=out, in_=in_, identity=ident)

        def act(out, in_, func, bias=None, scale=None, accum_out=None):
            kw = dict(out=out, in_=in_, func=func)
            if bias is not None:
                kw["bias"] = bias
            if scale is not None:
                kw["scale"] = scale
            if accum_out is not None:
                kw["accum_out"] = accum_out
            p.I("act", "activation", **kw)

        def tt(eng, out, in0, in1, op):
            p.I(eng, "tensor_tensor", out=out, in0=in0, in1=in1, op=op)

        def ts(eng, out, in0, s1, op0, s2=None, op1=None):
            kw = dict(out=out, in0=in0, scalar1=s1, scalar2=s2, op0=op0)
            if op1 is not None:
                kw["op1"] = op1
            p.I(eng, "tensor_scalar", **kw)

        def stt(out, in0, scalar, in1, op0, op1):
            p.I("dve", "scalar_tensor_tensor", out=out, in0=in0, scalar=scalar, in1=in1, op0=op0, op1=op1)

        def cp(eng, out, in_):
            if eng == "act":
                act(out, in_, AF.Copy)
            else:
                p.I(eng, "tensor_copy", out=out, in_=in_)

        def memset(eng, ap, val):
            p.I(eng, "memset", ap=ap, constant=val)

        def dbg_dump(name, ap, shape):
            if name not in dbg:
                return
            cnt = dbg_outs.setdefault(name, [])
            idx = len(cnt)
            d = nc.dram_tensor(f"dbg_{name}_{idx}", list(shape), ap.dtype, kind="ExternalOutput").ap()
            cnt.append(f"dbg_{name}_{idx}")
            p.dma("sp", d, ap, key=f"dbg{len(p.dma_counts)}", is_output=True)

        cst = alloc([NCONST])
        ident = cst[:, C_ID:C_ID + 128]
        Umat = cst[:, C_U:C_U + 128]
        M1S = cst[:, C_M1S:C_M1S + 128]
        M2 = cst[:, C_M2:C_M2 + 128]
        Jm = cst[:, C_J:C_J + 128]
        identb = alloc([128], BF16)
        mk = alloc([7, 128], BF16)
        onesb = alloc([128], BF16)
        vec = alloc([NV])
        rowb = alloc([16])
        negexpalog = alloc([4])
        expsink = alloc([8])
        modT = None
        cact = alloc([KC, NSEQ])
        biasT = [alloc([8, 128]), alloc([8, 128])]
        g1w = alloc([D])
        g2w = alloc([D])
        w1s = alloc([KC])
        w2s = alloc([KC])
        halo_dn = [alloc([128])[:, 0:36].rearrange("p (c j) -> p c j", j=3) for _ in range(2)]
        halo_ffn = [alloc([128])[:, 0:88].rearrange("p (c j) -> p c j", j=2) for _ in range(2)]
        Sst = alloc([4, 128])
        Sbf = [alloc([4, 128], BF16), alloc([4, 128], BF16)]
        kT = alloc([128 + T], BF16)
        vaug = alloc([NB + 1, 2, 66], BF16)
        xh = [alloc([NB, D]), alloc([NB, D])]
        uT = alloc([KC, T], BF16)
        NSLOT = 4
        wring = [alloc([4096], BF16) for _ in range(NSLOT)]
        small = Ring([64], F32, 12)

        units = []

        unit_len = {}
        unit_srcs = {}

        def add_unit(name, srcs, n=4096):
            unit_len[name] = n
            unit_srcs[name] = srcs
            u = dram("wsc_" + name, [128, 4096], BF16, kind="Internal")
            p.tracked_dram.add("wsc_" + name)
            units.append((name, u, srcs))
            return u

        def v3(u, a, b):
            return u[:, 0:a * b].rearrange("p (a b) -> p a b", a=a, b=b)

        def wv(w, c0, n):
            return w[:, c0:c0 + n].rearrange("(kc p) n -> p kc n", p=128)

        add_unit("q", [(lambda u, g=g, c=c: u[:, :].rearrange("p (kc c g h) -> p kc c g h", kc=8, c=4, g=2, h=64)[:, :, c, g, :],
                        wv(w_in, AQ + g * 256 + c * 64, 64)) for g in range(2) for c in range(4)])
        add_unit("kvg", [(lambda u: v3(u, 8, 264)[:, :, 0:256], wv(w_in, AK, 256)),
                         (lambda u: v3(u, 8, 264)[:, :, 256:264], wv(w_in, DBETA, 8))], n=8 * 264)
        for nm, c0 in (("dq", DQ), ("dk", DK), ("dv", DV), ("z", DZ), ("ga0", GA), ("ga1", GA + 512), ("gd0", GD), ("gd1", GD + 512)):
            add_unit(nm, [(lambda u: v3(u, 8, 512), wv(w_in, c0, 512))])
        for hf in range(2):
            add_unit(f"wa{hf}", [(lambda u: v3(u, 4, 512), w_a[:, hf * 512:(hf + 1) * 512].rearrange("(kc p) n -> p kc n", p=128))], n=2048)
            add_unit(f"wd{hf}", [(lambda u: v3(u, 4, 512), w_d[:, hf * 512:(hf + 1) * 512].rearrange("(kc p) n -> p kc n", p=128))], n=2048)
        add_unit("wo0", [(lambda u: v3(u, 8, 512), wv(w_o, 0, 512))])
        add_unit("wo1", [(lambda u: v3(u, 8, 512), wv(w_o, 512, 512))])
        for j in range(11):
            add_unit(f"up{j}", [(lambda u, two=two: u[:, :].rearrange("p (kc two n) -> p kc two n", kc=8, two=2, n=256)[:, :, two, :],
                                 wv(w_up, two * DFF + 256 * j, 256)) for two in range(2)])
        for j in range(6):
            nfc = min(4, NPAIR - 4 * j)
            add_unit(f"dn{j}", [(lambda u, nfc=nfc: v3(u, nfc, 1024),
                                 w_dn[4 * j * 128:(4 * j + nfc) * 128, :].rearrange("(fc p) n -> p fc n", p=128))], n=nfc * 1024)
        unit_ap = {name: u for name, u, _ in units}
        def convert(names, after=()):
            for name, u, srcs in units:
                if name in names:
                    for f, s_ in srcs:
                        p.dma("pool", f(u), s_, key="cv_" + name, after=after)

        def last_pe():
            return [p.ops["pe"][-1]] if p.ops["pe"] else []

        converted = set()
        LA = 7

        def load_unit(name):
            def ld(slot, key):
                n = unit_len[name]
                p.dma("sp", slot[:, 0:n], unit_ap[name][:, 0:n], key=key)
            return ld

        def load_unit_direct(name):
            def ld(slot, key):
                srcs_ = unit_srcs[name]
                for i_, (f, s_) in enumerate(srcs_):
                    p.dma("pool", f(slot), s_, key=key + "s", last=(i_ == len(srcs_) - 1))
                n = unit_len[name]
                p.dma("sp", unit_ap[name][:, 0:n], slot[:, 0:n], key="cv_" + name)
            return ld

        def load_ada(v, part):
            def ld(slot, key):
                c0 = v * 1024 + part * 512
                p.dma("pool", slot[:, :].rearrange("p (kc n) -> p kc n", kc=KC, n=512),
                      ada_w[:, c0:c0 + 512].rearrange("(kc p) n -> p kc n", p=128), key=key + "s")
            return ld

        class WStream:
            def __init__(self, seq):
                self.seq = seq
                self.issued = 0
                self.cur = 0

            def acquire(self, name):
                assert self.seq[self.cur][0] == name, (self.seq[self.cur][0], name)
                while self.issued < min(len(self.seq), self.cur + NSLOT - 1):
                    self.seq[self.issued][1](wring[self.issued % NSLOT], f"wr{self.issued % NSLOT}")
                    self.issued += 1
                slot = wring[self.cur % NSLOT]
                self.cur += 1
                return slot

        def ada_units(v):
            return [(f"ada{v}_{part}", load_ada(v, part)) for part in range(2)]

        base = ["q", "kvg", "dq", "dk", "dv", "z", "wa0", "ga0", "wa1", "ga1", "wd0", "gd0", "wd1", "gd1"]
        tail_o = ["wo0", "wo1"]
        ups = [f"up{j}" for j in range(11)]
        dns = [f"dn{j}" for j in range(6)]
        U_ = lambda names: [(n, load_unit(n)) for n in names]
        UD_ = lambda names: [(n, load_unit_direct(n)) for n in names]
        seq0 = ada_units(0) + ada_units(1) + UD_(base) + ada_units(2) + UD_(tail_o) + ada_units(3) + ada_units(4) + UD_(ups) + ada_units(5) + UD_(dns)
        seqn = U_(base + tail_o + ups + dns)
        W = WStream(seq0 + [e for _ in range(ntiles - 1) for e in seqn])

        p.dma("sp", cst, consts, key="c0")
        p.dma("sp", vec, vecs, key="c1")
        p.dma("sp", rowb, bass.AP(rowc.tensor, 0, [[0, 128], [1, 16]]), key="c2")
        p.dma("sp", cact, cT.rearrange("(kc p) s -> p kc s", p=128), key="c7")
        p.dma("sp", xh[0], x[0, 0:T, :].rearrange("(b p) d -> p b d", p=128), key="x0")
        cp("dve", identb, ident)
        memset("dve", onesb, 1.0)
        memset("dve", vaug[:, :, :, 64:66], 1.0)
        act(cact, cact, AF.Silu)
        act(negexpalog, rowb[:, 0:4], AF.Exp)
        ts("dve", negexpalog, negexpalog, -1.0, ALU.mult)
        act(expsink, rowb[:, 8:16], AF.Exp)

        modv = [alloc([128]) for _ in range(6)]
        r0 = alloc([4096])

        def modcol(v):
            return modv[v][:, 0:KC * NSEQ].rearrange("p (j s) -> p j s", s=NSEQ)

        cactb = alloc([KC, NSEQ], BF16)
        cp("dve", cactb, cact)

        def ada_vec(v):
            pT = psum()
            rows = r0[:, 0:512]
            for part in range(2):
                sl = v3(W.acquire(f"ada{v}_{part}"), KC, 512)
                pr = psum()
                for kc in range(KC):
                    mm(pr[0:NSEQ, :], cactb[:, kc, :], sl[:, kc, :], start=(kc == 0), stop=(kc == KC - 1))
                cp("dve", rows[0:NSEQ, :], pr[0:NSEQ, :])
                for j4 in range(4):
                    j = part * 4 + j4
                    tr(pT[:, NSEQ * j:NSEQ * (j + 1)], rows[0:NSEQ, j4 * 128:(j4 + 1) * 128], ident[0:NSEQ, 0:NSEQ])
            tt("dve", modcol(v), pT[:, 0:KC * NSEQ].rearrange("p (j s) -> p j s", s=NSEQ),
               vec[:, V_ADAB + 8 * v:V_ADAB + 8 * v + 8].unsqueeze(2).to_broadcast([128, KC, NSEQ]), ALU.add)

        ada_vec(0)
        ada_vec(1)

        ov_save = state["off"]
        raug = alloc([8])
        selsb = alloc([512])
        Esb = alloc([512])
        Hk = alloc([8, 128])
        lm_st = alloc([7 * 128])
        p.dma("sp", lm_st, lmask, key="c8")
        cp("dve", mk, lm_st[:, :].rearrange("p (l j) -> p l j", l=7))
        memset("dve", raug[32:33, :], -BIG)
        p.dma("sp", raug[0:32, :], relb, key="c3")
        p.dma("sp", selsb[0:33, :], selc, key="c4")
        pE = psum()
        mm(pE[0:8, :], raug[0:33, 0:8], selsb[0:33, :])
        cp("dve", Esb[0:8, :], pE[0:8, :])
        p.dma("sp", escr.rearrange("t h j -> h t j"), Esb[0:8, :].rearrange("h (t j) -> h t j", t=2), key="c5")
        for t_ in range(2):
            p.dma("sp", Hk, bass.AP(escr.tensor, t_ * 8 * 256, [[1, 128], [256, 8], [1, 128]]), key="c6")
            for hh in range(2):
                pb_ = psum()
                for h in range(4):
                    mm(pb_[:, h * 128:(h + 1) * 128], Hk[:, hh * 4 + h, :], Jm)
                cp("dve", biasT[t_][:, hh * 4:(hh + 1) * 4, :], pb_[:, :].rearrange("p (h q) -> p h q", h=4))
        state["off"] = ov_save

        OV = state["off"]
        xn = [r0[:, b * 1024:(b + 1) * 1024] for b in range(NB)]
        DK_ = 2
        dF = [[r0[:, (2 * j + i) * 512:(2 * j + i + 1) * 512].rearrange("p (h i) -> p h i", h=4) for i in range(2)] for j in range(DK_)]
        m1 = r0[:, 2048:4096].bitcast(BF16).rearrange("p (c t) -> p c t", c=KC)
        junkF = r0[:, 0:256].bitcast(BF16)
        OV1 = state["off"]
        qT = alloc([4, T], BF16)
        yaT = alloc([4, T], BF16)
        dqk = alloc([KC, T], BF16)
        dqT = dqk[:, 0:4, :]
        dkT = dqk[:, 4:8, :]
        dvT = alloc([4, T], BF16)
        szT = alloc([4, T], BF16)
        ydT = alloc([4, T], BF16)
        mT = dqk
        gcol = alloc([NB, 4])
        bcol = alloc([NB, 4])
        gtmp = alloc([NB, 4])
        OV2 = state["off"]
        ends = []
        BK_ = 9
        Bset = [dict(acc=alloc([T]), sl=alloc([T]), sqb=alloc([T], BF16)) for _ in range(BK_)]
        ends.append(state["off"]); state["off"] = OV2
        sc_r = Ring([512], F32, 2)
        pT_r = Ring([4, 128], BF16, 4)
        yat_r = Ring([512], BF16, 2)
        sg_r = Ring([512], F32, 2)
        DN_NAMES = ["ktok", "vtok", "qd", "AqkT", "A", "Ua", "Ub", "Va", "Vb", "Yp", "Al0", "Al1", "TbT", "kw", "ktl", "nwT"]
        Dset = []
        for j in range(DK_):
            d_ = {n: alloc([4, 128], BF16) for n in DN_NAMES}
            d_["sm"] = alloc([64])
            d_["F0"], d_["F1"] = dF[j]
            Dset.append(d_)
        ends.append(state["off"]); state["off"] = OV2
        ytmp_r = Ring([D], F32, 2)
        ends.append(state["off"])
        OV_MIX_END = max(ends)
        state["off"] = OV1
        HK_ = 4
        Hset = [dict(a0=alloc([T]), a1=alloc([T]), tmp=alloc([T])) for _ in range(HK_)]
        hidT = alloc([NPAIR, T], BF16)
        ytmp2_r = Ring([D], F32, 2)
        state["off"] = max(state["off"], OV_MIX_END)
        print("SBUF arena bytes used:", state["off"], "of", ARENA_E * 2)

        def VC(base, c):
            return vec[:, base + c:base + c + 1]

        def run_tasks(tasks, admit_n=None):
            pending = list(tasks)
            active = []
            counters = {}
            admit_n = admit_n or {}
            while pending or active:
                admitted = {}
                rest = []
                for (cls, k, fn) in pending:
                    nact = sum(1 for a in active if a[0] == cls)
                    if admitted.get(cls, 0) < admit_n.get(cls, 1) and nact < k and admitted.get(cls, 0) >= 0:
                        c = counters.get(cls, 0)
                        counters[cls] = c + 1
                        active.append((cls, fn(c % k)))
                        admitted[cls] = admitted.get(cls, 0) + 1
                    else:
                        admitted[cls] = -1
                        rest.append((cls, k, fn))
                pending = rest
                still = []
                for cls, g in active:
                    try:
                        next(g)
                        still.append((cls, g))
                    except StopIteration:
                        pass
                active = still

        def rstd_from_ss(ss_ap, n, inv_n):
            r = small.next()[:, 0:n]
            ts("dve", r, ss_ap, inv_n, ALU.mult, RMS_EPS, ALU.add)
            act(r, r, AF.Sqrt)
            p.I("dve", "reciprocal", out=r, in_=r)
            return r

        def prenorm_part1(xt):
            ss = small.next()[:, 0:NB]
            for b in range(NB):
                act(xn[b][:, 0:512].bitcast(BF16), xt[:, b, :], AF.Square, accum_out=ss[:, b:b + 1])
            r = rstd_from_ss(ss, NB, 1.0 / D)
            for b in range(NB):
                if b % 2 == 0:
                    act(xn[b], xt[:, b, :], AF.Copy, scale=r[:, b:b + 1])
                else:
                    ts("dve", xn[b], xt[:, b, :], r[:, b:b + 1], ALU.mult)

        def prenorm_part2(wcol, shv, s):
            for kc in range(KC):
                ps = psum()
                for b in range(NB):
                    tr(ps[:, b * 128:(b + 1) * 128], xn[b][:, kc * 128:(kc + 1) * 128], ident)
                if kc % 2 == 0:
                    act(uT[:, kc, :], ps[:, :], AF.Identity, bias=modcol(shv)[:, kc, s:s + 1], scale=wcol[:, kc:kc + 1])
                else:
                    ts("dve", uT[:, kc, :], ps[:, :], wcol[:, kc:kc + 1], ALU.mult, modcol(shv)[:, kc, s:s + 1], ALU.add)

        def prenorm_to_uT(xt, wcol, shv, s):
            prenorm_part1(xt)
            prenorm_part2(wcol, shv, s)

        def postnorm_residual(xt, b, phs, gw, ytr, jk=None):
            jk = junkF if jk is None else jk
            ss2 = small.next()[:, 0:2]
            for hf in range(2):
                act(jk, phs[hf][:, :], AF.Square, accum_out=ss2[:, hf:hf + 1])
            ss = small.next()[:, 0:1]
            tt("dve", ss, ss2[:, 0:1], ss2[:, 1:2], ALU.add)
            r = rstd_from_ss(ss, 1, 1.0 / D)
            yt = ytr.next()
            for hf in range(2):
                stt(yt[:, hf * 512:(hf + 1) * 512], phs[hf][:, :], r[:, 0:1], gw[:, hf * 512:(hf + 1) * 512], ALU.mult, ALU.mult)
            tt("pool", xt[:, b, :], xt[:, b, :], yt, ALU.add)

        def setup_wcol(wdst, scv, vbase, s):
            stt(wdst, modcol(scv)[:, :, s], 1.0, vec[:, vbase:vbase + 8], ALU.add, ALU.mult)

        def setup_gw(gw, gv, vbase, s):
            gc_ = small.next()[:, 0:8]
            tt("dve", gc_, modcol(gv)[:, :, s], vec[:, vbase:vbase + 8], ALU.mult)
            for hf in range(2):
                ps = psum()
                for k4 in range(4):
                    kc = hf * 4 + k4
                    cb = xn[k4][:, 0:128]
                    cp("dve", cb, gc_[:, kc:kc + 1].to_broadcast([128, 128]))
                    mm(ps[:, k4 * 128:(k4 + 1) * 128], cb, ident)
                cp("act", gw[:, hf * 512:(hf + 1) * 512], ps[:, :])

        def stage(name):
            if stop == name:
                raise _Stop()

        tile_no = 0
        hoisted = {"done": False}
        try:
          stage('P')
          for s in range(NSEQ):
            if tile_no >= ntiles:
                break
            memset("pool", Sst, 0.0)
            memset("pool", Sbf[0], 0.0)
            memset("pool", halo_dn[tile_no % 2], 0.0)
            memset("pool", halo_ffn[tile_no % 2], 0.0)
            sbc = {"i": 0}
            for ti in range(NT):
                if tile_no >= ntiles:
                    break
                first = (ti == 0)
                xt = xh[tile_no % 2]
                state["nrot"] = 8
                nt_ = tile_no + 1
                if nt_ < ntiles:
                    ns, nti = divmod(nt_, NT)
                    p.dma("sp", xh[nt_ % 2], x[ns, nti * T:(nti + 1) * T, :].rearrange("(b p) d -> p b d", p=128), key=f"x{nt_ % 2}")
                if first:
                    setup_wcol(w1s, 1, V_NMP, s)
                if not hoisted["done"]:
                    prenorm_part1(xt)
                hoisted["done"] = False
                prenorm_part2(w1s, 0, s)
                if "uT" in dbg and tile_no == 0:
                    dbg_dump("uT", uT, [128, KC, T])
                stage('A')
                sl = W.acquire("q")
                wq = v3(sl, 8, 512)
                for c in range(4):
                    ps = psum()
                    for kc in range(KC):
                        mm(ps[:, :], wq[:, kc, c * 128:(c + 1) * 128], uT[:, kc, :], start=(kc == 0), stop=(kc == KC - 1))
                    act(qT[:, c, :], ps[:, :], AF.Copy, scale=0.125)
                sl = W.acquire("kvg")
                wk = v3(sl, 8, 264)
                ps = psum()
                for kc in range(KC):
                    mm(ps[:, :], wk[:, kc, 0:128], uT[:, kc, :], start=(kc == 0), stop=(kc == KC - 1))
                cp("act", kT[:, 128:128 + T], ps[:, :])
                ps = psum()
                for b in range(NB):
                    for kc in range(KC):
                        mm(ps[:, b * 128:(b + 1) * 128], uT[:, kc, b * 128:(b + 1) * 128], wk[:, kc, 128:256], start=(kc == 0), stop=(kc == KC - 1))
                cp("dve", vaug[:, 1:NB + 1, :, 0:64], ps[:, :].rearrange("p (b k d) -> p b k d", b=NB, k=2))
                ps = psum()
                for b in range(NB):
                    for kc in range(KC):
                        mm(ps[:, b * 8:(b + 1) * 8], uT[:, kc, b * 128:(b + 1) * 128], wk[:, kc, 256:264], start=(kc == 0), stop=(kc == KC - 1))
                pg3 = ps[:, 0:NB * 8].rearrange("p (b c) -> p b c", c=8)
                act(bcol, pg3[:, :, 0:4], AF.Sigmoid)
                tt("dve", gtmp, pg3[:, :, 4:8], rowb[:, 4:8].unsqueeze(1).to_broadcast([128, NB, 4]), ALU.add)
                act(gtmp, gtmp, AF.Exp)
                act(gtmp, gtmp, AF.Ln, bias=1.0)
                tt("dve", gcol, gtmp, negexpalog[:, 0:4].unsqueeze(1).to_broadcast([128, NB, 4]), ALU.mult)
                wunit = {}

                def conv_task(ui, hh, dst):
                    def gen(j):
                        B_ = Bset[j]
                        ch = ui * 4 + hh
                        if hh == 0:
                            wunit[ui] = v3(W.acquire(("dq", "dk", "dv")[ui]), 8, 512)
                        wu = wunit[ui]
                        ps = psum()
                        for kc in range(KC):
                            mm(ps[:, :], wu[:, kc, hh * 128:(hh + 1) * 128], uT[:, kc, :], start=(kc == 0), stop=(kc == KC - 1))
                        acc = B_["acc"]
                        hold = halo_dn[tile_no % 2][:, ch, :]
                        act(acc, ps[:, :], AF.Copy, scale=VC(V_DNC, 3 * 12 + ch))
                        cp("act", halo_dn[(tile_no + 1) % 2][:, ch, :], ps[:, T - 3:T])
                        for jt in range(3):
                            sh = 3 - jt
                            wj = VC(V_DNC, jt * 12 + ch)
                            stt(acc[:, sh:T], ps[:, 0:T - sh], wj, acc[:, sh:T], ALU.mult, ALU.add)
                            stt(acc[:, 0:sh], hold[:, 3 - sh:3], wj, acc[:, 0:sh], ALU.mult, ALU.add)
                        yield
                        yield
                        if ui == 2:
                            act(dst[:, hh, :], acc, AF.Silu)
                            return
                        slv, sqb = B_["sl"], B_["sqb"]
                        act(slv, acc, AF.Silu)
                        tt("pool", sqb, slv, slv, ALU.mult)
                        yield
                        ps2 = psum()
                        mm(ps2[:, :], onesb, sqb)
                        act(acc, ps2[:, :], AF.Ln, bias=L2_EPS)
                        yield
                        act(acc, acc, AF.Exp, scale=-0.5, bias=(-0.5 * float(np.log(128.0)) if ui == 0 else 0.0))
                        tt("dve", dst[:, hh, :], slv, acc, ALU.mult)
                    return gen

                def attn_task(j):
                    for b in range(NB):
                        gb = ti * NB + b
                        kbs = ([0] if gb > 0 else []) + [1]
                        po = [psA[0], psA[1]]
                        for kv in range(2):
                            pts = {}
                            for kb in kbs:
                                ps = psum()
                                kcol = (b + kb) * 128
                                mm(ps[:, :], kT[64 * kv:64 * kv + 64, kcol:kcol + 128], qT[64 * kv:64 * kv + 64, :, b * 128:(b + 1) * 128])
                                sc = sc_r.next()
                                tt("dve", sc, ps[:, :], biasT[kb][:, 4 * kv:4 * kv + 4, :], ALU.add)
                                pt = pT_r.next()
                                act(pt, sc, AF.Exp)
                                pts[kb] = pt
                            for g in range(4):
                                for n_, kb in enumerate(kbs):
                                    mm(po[kv][:, g * 65:(g + 1) * 65], pts[kb][:, g, :], vaug[:, b + kb, kv, 0:65],
                                       start=(n_ == 0), stop=(n_ == len(kbs) - 1))
                            yield
                        yat = yat_r.next()
                        for kv in range(2):
                            pov = po[kv][:, 0:260].rearrange("p (g d) -> p g d", d=65)
                            den = small.next()[:, 0:4]
                            tt("dve", den, pov[:, :, 64], expsink[:, 4 * kv:4 * kv + 4], ALU.add)
                            p.I("dve", "reciprocal", out=den, in_=den)
                            tt("dve", yat[:, kv * 256:(kv + 1) * 256].rearrange("p (g d) -> p g d", d=64), pov[:, :, 0:64],
                               den.unsqueeze(2).to_broadcast([128, 4, 64]), ALU.mult)
                        yield
                        psb = psum()[:, :].bitcast(BF16)
                        for c in range(4):
                            tr(psb[:, c * 128:(c + 1) * 128], yat[:, c * 128:(c + 1) * 128], identb)
                        cp("act", yaT[:, :, b * 128:(b + 1) * 128], psb[:, 0:512].rearrange("p (c t) -> p c t", c=4))
                        yield
                    cp("pool", kT[:, 0:128], kT[:, T:T + 128])
                    cp("pool", vaug[:, 0, :, 0:64], vaug[:, NB, :, 0:64])

                def dn_task(b):
                    def gen(j):
                        S_ = Dset[j]
                        blk = slice(b * 128, (b + 1) * 128)
                        F0, F1 = S_["F0"], S_["F1"]
                        ktok, vtok = S_["ktok"], S_["vtok"]
                        sm = S_["sm"]
                        gc_c, ngc_c, egc, tail, glb, nbeta = [sm[:, 4 * i:4 * i + 4] for i in range(6)]
                        r4 = lambda t_: t_[:, :].rearrange("p (h i) -> p h i", h=4)
                        for (srcT, dstk, eng) in ((dkT, ktok, "act"), (dvT, vtok, "dve")):
                            psb = psum()[:, :].bitcast(BF16)
                            for h in range(4):
                                tr(psb[:, h * 128:(h + 1) * 128], srcT[:, h, blk], identb)
                            cp(eng, dstk, psb[:, 0:512].rearrange("p (h d) -> p h d", h=4))
                        pgc = psum()
                        mm(pgc[:, 0:4], Umat, gcol[:, b, :])
                        cp("dve", gc_c, pgc[:, 0:4])
                        ts("dve", ngc_c, gc_c, -1.0, ALU.mult)
                        cp("dve", F0, gcol[:, b, :].unsqueeze(2).to_broadcast([128, 4, 128]))
                        ts("dve", nbeta, bcol[:, b, :], -1.0, ALU.mult)
                        act(egc, gc_c, AF.Exp)
                        yield
                        pgcb = psum()
                        for h in range(4):
                            mm(pgcb[:, h * 128:(h + 1) * 128], F0[:, h, :], Umat)
                        pgcb3 = r4(pgcb)
                        act(F1, pgcb3, AF.Exp)
                        tt("dve", tail, pgcb3[:, :, 127], gc_c, ALU.subtract)
                        act(glb, pgcb3[:, :, 127], AF.Exp)
                        tt("dve", F0, pgcb3, M1S.unsqueeze(1).to_broadcast([128, 4, 128]), ALU.add)
                        tt("dve", S_["qd"], dqT[:, :, blk], F1, ALU.mult)
                        tt("dve", F1, pgcb3, M2.unsqueeze(1).to_broadcast([128, 4, 128]), ALU.add)
                        act(tail, tail, AF.Exp)
                        yield
                        for h in range(4):
                            act(F0[:, h, :], F0[:, h, :], AF.Exp, scale=-1.0, bias=gc_c[:, h:h + 1])
                            act(F1[:, h, :], F1[:, h, :], AF.Exp, scale=1.0, bias=ngc_c[:, h:h + 1])
                        pkk = psum()
                        for h in range(4):
                            mm(pkk[:, h * 128:(h + 1) * 128], dkT[:, h, blk], dkT[:, h, blk])
                        pqk = psum()
                        for h in range(4):
                            mm(pqk[:, h * 128:(h + 1) * 128], dkT[:, h, blk], dqT[:, h, blk])
                        A = S_["A"]
                        for h in range(4):
                            stt(A[:, h, :], pkk[:, h * 128:(h + 1) * 128], nbeta[:, h:h + 1], F0[:, h, :], ALU.mult, ALU.mult)
                        tt("dve", S_["AqkT"], r4(pqk), F1, ALU.mult)
                        tt("pool", S_["kw"], ktok, egc.unsqueeze(2).to_broadcast([128, 4, 128]), ALU.mult)
                        tt("pool", S_["ktl"], ktok, tail.unsqueeze(2).to_broadcast([128, 4, 128]), ALU.mult)
                        yield
                        mkb = lambda l: mk[:, l, :].unsqueeze(1).to_broadcast([128, 4, 128])
                        Al = [S_["Al0"], S_["Al1"]]
                        tt("pool", Al[0], A, mkb(0), ALU.mult)
                        yield
                        pat = psum()[:, :].bitcast(BF16)
                        for h in range(4):
                            tr(pat[:, h * 128:(h + 1) * 128], Al[0][:, h, :], identb)
                        pat3 = pat[:, 0:512].rearrange("p (h i) -> p h i", h=4)
                        Ub_ = [S_["Ua"], S_["Ub"]]
                        Vb_ = [S_["Va"], S_["Vb"]]
                        U, V = Ub_[0], Vb_[0]
                        tt("dve", V, pat3, identb.unsqueeze(1).to_broadcast([128, 4, 128]), ALU.add)
                        tt("pool", U, Al[0], identb.unsqueeze(1).to_broadcast([128, 4, 128]), ALU.add)
                        tt("pool", Al[1], A, mkb(1), ALU.mult)
                        yield
                        Yp = S_["Yp"]
                        for lvl in range(1, 7):
                            Acur = Al[lvl % 2]
                            pY = psum()
                            for h in range(4):
                                mm(pY[:, h * 128:(h + 1) * 128], Acur[:, h, :], V[:, h, :])
                            cp("act", Yp, r4(pY))
                            if lvl < 6:
                                tt("pool", Al[(lvl + 1) % 2], A, mkb(lvl + 1), ALU.mult)
                            yield
                            pu = psum()
                            for h in range(4):
                                mm(pu[:, h * 128:(h + 1) * 128], Yp[:, h, :], U[:, h, :])
                            pv = psum()
                            for h in range(4):
                                mm(pv[:, h * 128:(h + 1) * 128], U[:, h, :], Yp[:, h, :])
                            Un, Vn = Ub_[lvl % 2], Vb_[lvl % 2]
                            tt("dve", Un, r4(pu), U, ALU.add)
                            tt("dve", Vn, r4(pv), V, ALU.add)
                            U, V = Un, Vn
                            yield
                        X = V
                        TbT = S_["TbT"]
                        tt("dve", TbT, X, bcol[:, b, :].unsqueeze(2).to_broadcast([128, 4, 128]), ALU.mult)
                        yield
                        pw = psum()
                        for h in range(4):
                            mm(pw[:, h * 128:(h + 1) * 128], S_["kw"][:, h, :], TbT[:, h, :])
                        nwT = S_["nwT"]
                        act(nwT, r4(pw), AF.Copy, scale=-1.0)
                        yield
                        Sb = Sbf[sbc["i"] % 2]
                        pv = psum()
                        for h in range(4):
                            mm(pv[:, h * 128:(h + 1) * 128], TbT[:, h, :], vtok[:, h, :], start=True, stop=False)
                            mm(pv[:, h * 128:(h + 1) * 128], nwT[:, h, :], Sb[:, h, :], start=False, stop=True)
                        vnew = S_["kw"]
                        cp("act", vnew, pv[:, :].rearrange("p (h e) -> p h e", h=4))
                        yield
                        po_ = psum()
                        for h in range(4):
                            mm(po_[:, h * 128:(h + 1) * 128], Sb[:, h, :], S_["qd"][:, h, :], start=True, stop=False)
                            mm(po_[:, h * 128:(h + 1) * 128], vnew[:, h, :], S_["AqkT"][:, h, :], start=False, stop=True)
                        psu = psum()
                        for h in range(4):
                            mm(psu[:, h * 128:(h + 1) * 128], S_["ktl"][:, h, :], vnew[:, h, :])
                        for h in range(4):
                            stt(Sst[:, h, :], Sst[:, h, :], glb[:, h:h + 1], psu[:, h * 128:(h + 1) * 128], ALU.mult, ALU.add)
                        sbc["i"] += 1
                        cp("act", Sbf[sbc["i"] % 2], Sst)
                        sq = S_["A"]
                        act(sq, r4(po_), AF.Square)
                        cp("dve", F1, r4(po_))
                        yield
                        pss = psum()
                        mm(pss[:, :], onesb, sq[:, :, :].rearrange("p h i -> p (h i)"))
                        act(F0, r4(pss), AF.Ln, scale=1.0 / 128.0, bias=RMS_EPS)
                        yield
                        act(F0, F0, AF.Exp, scale=-0.5)
                        if "dn_o" in dbg and tile_no == 0 and b <= 1:
                            dbg_dump("dn_o", F1, [128, 4, 128])
                        tt("dve", F1, F1, F0, ALU.mult)
                        stt(ydT[:, :, blk], F1, VC(V_DNW, 0), szT[:, :, blk], ALU.mult, ALU.mult)
                    return gen

                def mergeA_task(j):
                    for hf in range(2):
                        wb_ = v3(W.acquire(f"wa{hf}"), 4, 512)
                        wg = v3(W.acquire(f"ga{hf}"), 8, 512)
                        for cc in range(4):
                            c = hf * 4 + cc
                            pg_ = psum()
                            for kc in range(KC):
                                mm(pg_[:, :], wg[:, kc, cc * 128:(cc + 1) * 128], uT[:, kc, :], start=(kc == 0), stop=(kc == KC - 1))
                            sg = sg_r.next()
                            act(sg, pg_[:, :], AF.Sigmoid)
                            yield
                            pb_ = psum()
                            for kc in range(4):
                                mm(pb_[:, :], wb_[:, kc, cc * 128:(cc + 1) * 128], yaT[:, kc, :], start=(kc == 0), stop=(kc == 3))
                            tt("dve", m1[:, c, :], pb_[:, :], sg, ALU.mult)
                            yield

                run_tasks([("B", BK_, conv_task(ui, hh, dst)) for ui, dst in enumerate((dqT, dkT, dvT)) for hh in range(4)], admit_n={"B": 2})
                sl = W.acquire("z")
                wu = v3(sl, 8, 512)
                for hh in range(4):
                    ps = psum()
                    for kc in range(KC):
                        mm(ps[:, :], wu[:, kc, hh * 128:(hh + 1) * 128], uT[:, kc, :], start=(kc == 0), stop=(kc == KC - 1))
                    act(szT[:, hh, :], ps[:, :], AF.Silu)
                stage('B')
                def attn_merge_task(j):
                    yield from attn_task(j)
                    yield
                    yield from mergeA_task(j)

                state["nrot"] = NROT
                run_tasks([("D", DK_, dn_task(b)) for b in range(NB)] + [("C", 1, attn_merge_task)])
                state["nrot"] = 8
                if "yaT" in dbg and tile_no <= 1:
                    dbg_dump("yaT", yaT, [128, 4, T])
                if "ydT" in dbg and tile_no <= 1:
                    dbg_dump("ydT", ydT, [128, 4, T])
                stage('D')
                for hf in range(2):
                    wb_ = v3(W.acquire(f"wd{hf}"), 4, 512)
                    wg = v3(W.acquire(f"gd{hf}"), 8, 512)
                    for cc in range(4):
                        c = hf * 4 + cc
                        pb_ = psum()
                        for kc in range(4):
                            mm(pb_[:, :], wb_[:, kc, cc * 128:(cc + 1) * 128], ydT[:, kc, :], start=(kc == 0), stop=(kc == 3))
                        pg_ = psum()
                        for kc in range(KC):
                            mm(pg_[:, :], wg[:, kc, cc * 128:(cc + 1) * 128], uT[:, kc, :], start=(kc == 0), stop=(kc == KC - 1))
                        sg = sg_r.next()
                        act(sg, pg_[:, :], AF.Sigmoid)
                        tt("dve", sg, pb_[:, :], sg, ALU.mult)
                        tt("pool", mT[:, c, :], sg, m1[:, c, :], ALU.add)
                if "mT" in dbg and tile_no == 0:
                    dbg_dump("mT", mT, [128, KC, T])
                stage('E')
                if first:
                    if tile_no == 0:
                        ada_vec(2)
                    setup_gw(g1w, 2, V_NMPOST, s)
                wo = [v3(W.acquire("wo0"), 8, 512), v3(W.acquire("wo1"), 8, 512)]
                for b in range(NB):
                    phs = [psum(), psum()]
                    for hf in range(2):
                        for kc in range(KC):
                            mm(phs[hf][:, :], mT[:, kc, b * 128:(b + 1) * 128], wo[hf][:, kc, :], start=(kc == 0), stop=(kc == KC - 1))
                    postnorm_residual(xt, b, phs, g1w, ytmp_r)
                if "h1" in dbg and tile_no == 0:
                    dbg_dump("h1", xt, [128, NB, D])
                stage('F')
                if first:
                    if tile_no == 0:
                        ada_vec(3)
                        ada_vec(4)
                    setup_wcol(w2s, 4, V_NFP, s)
                prenorm_to_uT(xt, w2s, 3, s)
                stage('G')
                wunit = {}

                def ffn_task(pair):
                    def gen(j):
                        H_ = Hset[j]
                        jj, pp_ = divmod(pair, 2)
                        if pp_ == 0:
                            wunit[jj] = W.acquire(f"up{jj}")[:, :].rearrange("p (kc two n) -> p kc two n", kc=8, two=2, n=256)
                        wu = wunit[jj]
                        for two in range(2):
                            ps = psum()
                            for kc in range(KC):
                                mm(ps[:, :], wu[:, kc, two, pp_ * 128:(pp_ + 1) * 128], uT[:, kc, :], start=(kc == 0), stop=(kc == KC - 1))
                            ch = two * NPAIR + pair
                            acc = H_[f"a{two}"]
                            hold = halo_ffn[tile_no % 2][:, ch, :]
                            wc = lambda jt: VC(V_FFC, jt * 2 * NPAIR + ch)
                            act(acc, ps[:, :], AF.Copy, scale=wc(2))
                            cp("act", halo_ffn[(tile_no + 1) % 2][:, ch, :], ps[:, T - 2:T])
                            if two == 1:
                                tmp = H_["tmp"]
                                act(tmp[:, 1:T], ps[:, 0:T - 1], AF.Copy, scale=wc(1))
                                act(tmp[:, 0:1], hold[:, 1:2], AF.Copy, scale=wc(1))
                            for jt in range(2 if two == 0 else 1):
                                sh = 2 - jt
                                stt(acc[:, sh:T], ps[:, 0:T - sh], wc(jt), acc[:, sh:T], ALU.mult, ALU.add)
                                stt(acc[:, 0:sh], hold[:, 2 - sh:2], wc(jt), acc[:, 0:sh], ALU.mult, ALU.add)
                            if two == 1:
                                tt("pool", acc, acc, tmp, ALU.add)
                        yield
                        yield
                        act(H_["a0"], H_["a0"], AF.Gelu_apprx_tanh)
                        tt("pool", hidT[:, pair, :], H_["a0"], H_["a1"], ALU.mult)
                    return gen

                run_tasks([("H", HK_, ffn_task(pair)) for pair in range(NPAIR)])
                stage('H')
                if first:
                    if tile_no == 0:
                        ada_vec(5)
                    setup_gw(g2w, 5, V_NFPOST, s)
                banks = list(psums)
                for j in range(6):
                    nfc = min(4, NPAIR - 4 * j)
                    sl = W.acquire(f"dn{j}")
                    wdn_ = v3(sl, nfc, 1024)
                    for fcl in range(nfc):
                        fc = 4 * j + fcl
                        for b in range(NB):
                            for hf in range(2):
                                mm(banks[b * 2 + hf][:, :], hidT[:, fc, b * 128:(b + 1) * 128], wdn_[:, fcl, hf * 512:(hf + 1) * 512],
                                   start=(fc == 0), stop=(fc == NPAIR - 1))
                if nt_ < ntiles:
                    prenorm_part1(xh[nt_ % 2])
                    hoisted["done"] = True
                jk2 = Hset[0]["tmp"][:, 0:256].bitcast(BF16)
                for b in range(NB):
                    postnorm_residual(xt, b, [banks[b * 2], banks[b * 2 + 1]], g2w, ytmp2_r, jk=jk2)
                p.dma("sp", out[s, ti * T:(ti + 1) * T, :].rearrange("(b p) d -> p b d", p=128), xt, key=f"o{tile_no % 2}", is_output=True)
                tile_no += 1
        except _Stop:
            pass
        p.emit()
    return nc, dbg_outs, p


_CACHE = {}


def _layout_inputs(inputs, core):
    f = lambda a: np.ascontiguousarray(np.asarray(a, dtype=np.float32))
    b0 = core * NSEQ
    vecs = np.zeros((128, NV), np.float32)
    col = lambda v: np.asarray(v, np.float32).reshape(-1, 128).T
    vecs[:, V_ADAB:V_ADAB + 48] = col(inputs["ada_b"][0])
    vecs[:, V_NMP:V_NMP + 8] = col(inputs["norm_mix_pre"][0])
    vecs[:, V_NMPOST:V_NMPOST + 8] = col(inputs["norm_mix_post"][0])
    vecs[:, V_NFP:V_NFP + 8] = col(inputs["norm_ffn_pre"][0])
    vecs[:, V_NFPOST:V_NFPOST + 8] = col(inputs["norm_ffn_post"][0])
    dnc = np.asarray(inputs["dn_conv_w"][0], np.float32)
    for j in range(4):
        vecs[:, V_DNC + j * 12:V_DNC + (j + 1) * 12] = col(dnc[j])
    ffc = np.asarray(inputs["ffn_conv_w"][0], np.float32)
    for j in range(3):
        vecs[:, V_FFC + j * 44:V_FFC + (j + 1) * 44] = col(ffc[j])
    vecs[:, V_DNW] = np.asarray(inputs["dn_norm_w"][0], np.float32)
    rowc = np.concatenate([np.asarray(inputs["dn_a_log"][0], np.float32), np.asarray(inputs["dn_dt_bias"][0], np.float32),
                           np.asarray(inputs["attn_sinks"][0], np.float32)])[None, :]
    return {
        "x": f(inputs["x"][b0:b0 + NSEQ]),
        "cT": f(np.asarray(inputs["c"][b0:b0 + NSEQ]).T),
        "vecs": vecs,
        "rowc": f(rowc),
    }


def kernel(**inputs):
    n = 8
    if "nc" not in _CACHE:
        _CACHE["nc"] = build_program()[0]
        _CACHE["consts"] = _host_consts()
    nc = _CACHE["nc"]
    consts, selc, lmask = _CACHE["consts"]
    f = lambda a: np.ascontiguousarray(np.asarray(a, dtype=np.float32))
    shared = {
        "ada_w": f(inputs["ada_w"][0]), "w_in": f(inputs["w_in"][0]), "w_a": f(inputs["w_attn_branch"][0]),
        "w_d": f(inputs["w_dn_branch"][0]), "w_o": f(inputs["w_out"][0]), "w_up": f(inputs["ffn_w_up"][0]),
        "w_dn": f(inputs["ffn_w_down"][0]), "relb": f(inputs["rel_bias"]), "consts": consts, "selc": selc, "lmask": lmask,
    }
    in_maps = []
    for c in range(n):
        m = dict(shared)
        m.update(_layout_inputs(inputs, c))
        in_maps.append(m)
    res = run_bass_kernel_spmd(nc, in_maps, core_ids=list(range(n)))
    return np.concatenate([np.asarray(r["out"]) for r in res.results], axis=0).astype(np.float32)
```

```python
import contextlib
import numpy as np
import concourse.bass as bass
import concourse.mybir as mybir
from concourse.bass_utils import run_bass_kernel_spmd

F32 = mybir.dt.float32
BF16 = mybir.dt.bfloat16
AF = mybir.ActivationFunctionType
ALU = mybir.AluOpType
ESZ = {F32: 4, BF16: 2}
PAGE = 512
ENGS = ("pe", "act", "dve", "pool", "sp")

D = 1024
KC = 8
SEQ = 2048
NSEQ = 2
T = 512
NB = T // 128
NT = SEQ // T
DFF = 2816
NPAIR = DFF // 128
RMS_EPS = 1e-6
L2_EPS = 1e-6
BIG = 30000.0
AQ, AK, AV, DQ, DK, DV, DZ, DBETA, DA, GA, GD = 0, 512, 640, 768, 1280, 1792, 2304, 2816, 2820, 2824, 3848
V_ADAB, V_NMP, V_NMPOST, V_NFP, V_NFPOST, V_DNC, V_FFC, V_DNW, NV = 0, 48, 56, 64, 72, 80, 128, 260, 261
C_ID, C_U, C_M1S, C_M2, C_J, NCONST = 0, 128, 256, 384, 512, 640


class Buf:
    __slots__ = ("w", "r")

    def __init__(self):
        self.w = {}
        self.r = {}


class Op:
    __slots__ = ("eng", "fn", "deps", "stream", "count", "is_dma", "needed")


def _is_ap(v):
    return hasattr(v, "tensor") and hasattr(v, "ap") and hasattr(v, "offset")


class Prog:
    def __init__(self, nc):
        self.nc = nc
        self.ops = {e: [] for e in ENGS}
        self.all_ops = []
        self.bufs = {}
        self.dma_counts = {}
        self.tracked_dram = set()
        self.out_dma = []
        self.dma_valid = {}

    def keys(self, ap):
        sp = str(ap.space)
        name = ap.tensor.name
        if sp == "PSUM":
            return [("P", name)]
        if sp == "DRAM":
            return [("D", name)] if name in self.tracked_dram else []
        esz = ESZ[ap.dtype]
        rowlen = 1
        for s in ap.tensor.shape[1:]:
            rowlen *= s
        off = ap.offset % rowlen
        dims = [(s, c) for (s, c) in list(ap.ap)[1:] if c > 1]
        pages = set()

        def rec(ds, base):
            if ds and abs(ds[0][0]) * esz >= 2 * PAGE and ds[0][1] <= 64:
                s, c = ds[0]
                for i in range(c):
                    rec(ds[1:], base + i * s)
                return
            lo = hi = base
            for s, c in ds:
                e = s * (c - 1)
                if e < 0:
                    lo += e
                else:
                    hi += e
            for pg in range((lo * esz) // PAGE, (hi * esz + esz - 1) // PAGE + 1):
                pages.add(pg)

        rec(dims, off)
        return [("S", name, pg) for pg in pages]

    def _buf(self, k):
        b = self.bufs.get(k)
        if b is None:
            b = self.bufs[k] = Buf()
        return b

    def _record(self, op, reads, writes):
        pe = op.eng == "pe" and not op.is_dma
        deps = {}
        rb = [self._buf(k) for ap in reads if str(ap.space) != "PSUM" for k in self.keys(ap)]
        wb = [self._buf(k) for ap in writes for k in self.keys(ap)]
        wb += [self._buf(k) for ap in reads if str(ap.space) == "PSUM" for k in self.keys(ap)]
        for b in rb:
            for s, c in b.w.items():
                if deps.get(s, 0) < c:
                    deps[s] = c
        for b in wb:
            for s, c in b.w.items():
                if pe and s == "pe":
                    continue
                if deps.get(s, 0) < c:
                    deps[s] = c
            for s, c in b.r.items():
                if deps.get(s, 0) < c:
                    deps[s] = c
        op.deps = deps
        self.ops[op.eng].append(op)
        self.all_ops.append(op)
        return rb, wb

    def _commit(self, op, rb, wb):
        s, c = op.stream, op.count
        for b in rb:
            if b.r.get(s, 0) < c:
                b.r[s] = c
        for b in wb:
            b.w = {s: c}
            b.r = {}

    def I(self, eng, meth, **kw):
        reads, writes = [], []
        for k, v in kw.items():
            if _is_ap(v):
                (writes if k in ("out", "accum_out", "ap") else reads).append(v)
        op = Op()
        op.eng = eng
        op.is_dma = False
        op.needed = False
        op.fn = lambda e, meth=meth, kw=kw: getattr(e, meth)(**kw)
        rb, wb = self._record(op, reads, writes)
        op.stream = eng
        op.count = len(self.ops[eng])
        self._commit(op, rb, wb)
        return op

    def dma(self, eng, out, in_, key, is_output=False, after=(), last=True, **kw):
        op = Op()
        op.eng = eng
        op.is_dma = True
        op.needed = False
        op.fn = lambda e, out=out, in_=in_, kw=kw: e.dma_start(out=out, in_=in_, **kw)
        rb, wb = self._record(op, [in_], [out])
        for o_ in after:
            if op.deps.get(o_.stream, 0) < o_.count:
                op.deps[o_.stream] = o_.count
        op.deps.pop("dma:" + key, None)
        c = self.dma_counts.get(key, 0) + 16
        self.dma_counts[key] = c
        op.stream = "dma:" + key
        op.count = c
        if last:
            self.dma_valid.setdefault(op.stream, []).append(c)
        self._commit(op, rb, wb)
        if is_output:
            self.out_dma.append((op.stream, c))
        return op

    def emit(self):
        nc = self.nc
        for o in self.all_ops:
            for s, c in o.deps.items():
                if not s.startswith("dma:"):
                    self.ops[s][c - 1].needed = True
        final = {}
        for e in ENGS:
            n = 0
            for i, o in enumerate(self.ops[e]):
                if not o.is_dma and o.needed:
                    n += 1
                    final[(e, i + 1)] = n
        with contextlib.ExitStack() as stack:
            sems = {}
            for e in ENGS:
                sems[e] = stack.enter_context(nc.semaphore("s_" + e))
            for k in self.dma_counts:
                sems["dma:" + k] = stack.enter_context(nc.semaphore("d_" + k))
            block = stack.enter_context(nc.Block())
            engmap = {"pe": block.tensor, "act": block.scalar, "dve": block.vector,
                      "pool": block.gpsimd, "sp": block.sync}

            def make(e):
                def body(eng):
                    waited = {}
                    for o in self.ops[e]:
                        for s, c in o.deps.items():
                            if s.startswith("dma:"):
                                v = next(x for x in self.dma_valid[s] if x >= c)
                            else:
                                v = final[(s, c)]
                            if waited.get(s, 0) < v:
                                eng.wait_ge(sems[s], v)
                                waited[s] = v
                        ins = o.fn(eng)
                        if o.is_dma:
                            ins.then_inc(sems[o.stream], 16)
                        elif o.needed:
                            ins.then_inc(sems[e], 1)
                    if e == "sp":
                        for s, c in self.out_dma:
                            if waited.get(s, 0) < c:
                                eng.wait_ge(sems[s], c)
                                waited[s] = c
                return body

            for e in ENGS:
                engmap[e](make(e))


def _t5_bucket(dist):
    dist = np.maximum(dist, 0)
    max_exact = 16
    scaled = np.log(np.maximum(dist, 1).astype(np.float32) / np.float32(max_exact)) / np.float32(np.log(128 / 16))
    large = max_exact + (scaled.astype(np.float32) * np.float32(16)).astype(np.int32)
    large = np.minimum(large, 31)
    return np.where(dist < max_exact, dist, large)


def _host_consts():
    c = np.zeros((128, NCONST), np.float32)
    i = np.arange(128)
    c[:, C_ID:C_ID + 128] = np.eye(128, dtype=np.float32)
    c[:, C_U:C_U + 128] = (i[:, None] <= i[None, :]).astype(np.float32)
    c[:, C_M1S:C_M1S + 128] = np.where(i[None, :] >= i[:, None], BIG, 0.0)
    c[:, C_M2:C_M2 + 128] = np.where(i[None, :] < i[:, None], -BIG, 0.0)
    c[:, C_J:C_J + 128] = (i[:, None] + i[None, :] == 127).astype(np.float32)
    sel = np.zeros((33, 512), np.float32)
    for jp in range(255):
        if jp >= 128:
            sel[_t5_bucket(np.array(255 - jp)), jp] = 1.0
        else:
            sel[32, jp] = 1.0
        if jp <= 127:
            sel[_t5_bucket(np.array(127 - jp)), 256 + jp] = 1.0
        else:
            sel[32, 256 + jp] = 1.0
    lm = np.zeros((128, 7 * 128), np.float32)
    for l in range(7):
        n = 1 << l
        m = ((i[:, None] // (2 * n)) == (i[None, :] // (2 * n))) & ((i[:, None] % (2 * n)) >= n) & ((i[None, :] % (2 * n)) < n)
        lm[:, l * 128:(l + 1) * 128] = m
    return c, sel, lm


class _Stop(Exception):
    pass


def build_program(ntiles=NSEQ * NT, dbg=(), stop=None):
    nc = bass.Bass("TRN2", target_bir_lowering=False)
    p = Prog(nc)

    def dram(name, shape, dt=F32, kind="ExternalInput"):
        return nc.dram_tensor(name, list(shape), dt, kind=kind).ap()

    x = dram("x", [NSEQ, SEQ, D])
    cT = dram("cT", [D, NSEQ])
    ada_w = dram("ada_w", [D, 6 * D])
    vecs = dram("vecs", [128, NV])
    rowc = dram("rowc", [1, 16])
    w_in = dram("w_in", [D, 4872])
    w_a = dram("w_a", [512, D])
    w_d = dram("w_d", [512, D])
    w_o = dram("w_o", [D, D])
    w_up = dram("w_up", [D, 2 * DFF])
    w_dn = dram("w_dn", [DFF, D])
    relb = dram("relb", [32, 8])
    consts = dram("consts", [128, NCONST])
    selc = dram("selc", [33, 512])
    lmask = dram("lmask", [128, 7 * 128])
    out = dram("out", [NSEQ, SEQ, D], kind="ExternalOutput")
    escr = dram("escr", [2, 8, 256], kind="Internal")
    p.tracked_dram.add("escr")
    dbg_outs = {}

    with contextlib.ExitStack() as st:
        ARENA_E = 104448
        arena = st.enter_context(nc.sbuf_tensor("arena", [128, ARENA_E], BF16))
        psums = [st.enter_context(nc.psum_tensor(f"ps{i}", [128, 512], F32)) for i in range(8)]
        state = {"off": 0, "ps": 0}

        def alloc(shape, dt=F32):
            n = 1
            for s in shape:
                n *= s
            nbytes = n * ESZ[dt]
            off = state["off"]
            al = PAGE if nbytes >= PAGE else 64
            off = (off + al - 1) // al * al
            state["off"] = off + nbytes
            assert state["off"] <= ARENA_E * 2, ("SBUF arena overflow", state["off"])
            v = arena[:, off // 2: (off + nbytes) // 2]
            if dt == F32:
                v = v.bitcast(F32)
            if len(shape) == 2:
                v = v.rearrange("p (a b) -> p a b", a=shape[0], b=shape[1])
            elif len(shape) == 3:
                v = v.rearrange("p (a b c) -> p a b c", a=shape[0], b=shape[1], c=shape[2])
            return v

        class Ring:
            def __init__(self, shape, dt, n):
                self.t = [alloc(shape, dt) for _ in range(n)]
                self.i = 0

            def next(self):
                v = self.t[self.i % len(self.t)]
                self.i += 1
                return v

        NROT = 6
        psA = [psums[6], psums[7]]

        state["nrot"] = NROT

        def psum():
            t = psums[state["ps"] % state["nrot"]]
            state["ps"] += 1
            return t

        def mm(out, lhsT, rhs, start=True, stop=True):
            p.I("pe", "matmul", out=out, lhsT=lhsT, rhs=rhs, start=start, stop=stop)

        def tr(out, in_, ident):
            p.I("pe", "transpose", out=out, in_=in_, identity=ident)

        def act(out, in_, func, bias=None, scale=None, accum_out=None):
            kw = dict(out=out, in_=in_, func=func)
            if bias is not None:
                kw["bias"] = bias
            if scale is not None:
                kw["scale"] = scale
            if accum_out is not None:
                kw["accum_out"] = accum_out
            p.I("act", "activation", **kw)

        def tt(eng, out, in0, in1, op):
            p.I(eng, "tensor_tensor", out=out, in0=in0, in1=in1, op=op)

        def ts(eng, out, in0, s1, op0, s2=None, op1=None):
            kw = dict(out=out, in0=in0, scalar1=s1, scalar2=s2, op0=op0)
            if op1 is not None:
                kw["op1"] = op1
            p.I(eng, "tensor_scalar", **kw)

        def stt(out, in0, scalar, in1, op0, op1):
            p.I("dve", "scalar_tensor_tensor", out=out, in0=in0, scalar=scalar, in1=in1, op0=op0, op1=op1)

        def cp(eng, out, in_):
            if eng == "act":
                act(out, in_, AF.Copy)
            else:
                p.I(eng, "tensor_copy", out=out, in_=in_)

        def memset(eng, ap, val):
            p.I(eng, "memset", ap=ap, constant=val)

        def dbg_dump(name, ap, shape):
            if name not in dbg:
                return
            cnt = dbg_outs.setdefault(name, [])
            idx = len(cnt)
            d = nc.dram_tensor(f"dbg_{name}_{idx}", list(shape), ap.dtype, kind="ExternalOutput").ap()
            cnt.append(f"dbg_{name}_{idx}")
            p.dma("sp", d, ap, key=f"dbg{len(p.dma_counts)}", is_output=True)

        cst = alloc([NCONST])
        ident = cst[:, C_ID:C_ID + 128]
        Umat = cst[:, C_U:C_U + 128]
        M1S = cst[:, C_M1S:C_M1S + 128]
        M2 = cst[:, C_M2:C_M2 + 128]
        Jm = cst[:, C_J:C_J + 128]
        identb = alloc([128], BF16)
        mk = alloc([7, 128], BF16)
        onesb = alloc([128], BF16)
        vec = alloc([NV])
        rowb = alloc([16])
        negexpalog = alloc([4])
        expsink = alloc([8])
        modT = None
        cact = alloc([KC, NSEQ])
        biasT = [alloc([8, 128]), alloc([8, 128])]
        g1w = alloc([D])
        g2w = alloc([D])
        w1s = alloc([KC])
        w2s = alloc([KC])
        halo_dn = [alloc([128])[:, 0:36].rearrange("p (c j) -> p c j", j=3) for _ in range(2)]
        halo_ffn = [alloc([128])[:, 0:88].rearrange("p (c j) -> p c j", j=2) for _ in range(2)]
        Sst = alloc([4, 128])
        Sbf = [alloc([4, 128], BF16), alloc([4, 128], BF16)]
        kT = alloc([128 + T], BF16)
        vaug = alloc([NB + 1, 2, 66], BF16)
        xh = [alloc([NB, D]), alloc([NB, D])]
        uT = alloc([KC, T], BF16)
        NSLOT = 4
        wring = [alloc([4096], BF16) for _ in range(NSLOT)]
        small = Ring([64], F32, 12)

        units = []

        unit_len = {}
        unit_srcs = {}

        def add_unit(name, srcs, n=4096):
            unit_len[name] = n
            unit_srcs[name] = srcs
            u = dram("wsc_" + name, [128, 4096], BF16, kind="Internal")
            p.tracked_dram.add("wsc_" + name)
            units.append((name, u, srcs))
            return u

        def v3(u, a, b):
            return u[:, 0:a * b].rearrange("p (a b) -> p a b", a=a, b=b)

        def wv(w, c0, n):
            return w[:, c0:c0 + n].rearrange("(kc p) n -> p kc n", p=128)

        add_unit("q", [(lambda u, g=g, c=c: u[:, :].rearrange("p (kc c g h) -> p kc c g h", kc=8, c=4, g=2, h=64)[:, :, c, g, :],
                        wv(w_in, AQ + g * 256 + c * 64, 64)) for g in range(2) for c in range(4)])
        add_unit("kvg", [(lambda u: v3(u, 8, 264)[:, :, 0:256], wv(w_in, AK, 256)),
                         (lambda u: v3(u, 8, 264)[:, :, 256:264], wv(w_in, DBETA, 8))], n=8 * 264)
        for nm, c0 in (("dq", DQ), ("dk", DK), ("dv", DV), ("z", DZ), ("ga0", GA), ("ga1", GA + 512), ("gd0", GD), ("gd1", GD + 512)):
            add_unit(nm, [(lambda u: v3(u, 8, 512), wv(w_in, c0, 512))])
        for hf in range(2):
            add_unit(f"wa{hf}", [(lambda u: v3(u, 4, 512), w_a[:, hf * 512:(hf + 1) * 512].rearrange("(kc p) n -> p kc n", p=128))], n=2048)
            add_unit(f"wd{hf}", [(lambda u: v3(u, 4, 512), w_d[:, hf * 512:(hf + 1) * 512].rearrange("(kc p) n -> p kc n", p=128))], n=2048)
        add_unit("wo0", [(lambda u: v3(u, 8, 512), wv(w_o, 0, 512))])
        add_unit("wo1", [(lambda u: v3(u, 8, 512), wv(w_o, 512, 512))])
        for j in range(11):
            add_unit(f"up{j}", [(lambda u, two=two: u[:, :].rearrange("p (kc two n) -> p kc two n", kc=8, two=2, n=256)[:, :, two, :],
                                 wv(w_up, two * DFF + 256 * j, 256)) for two in range(2)])
        for j in range(6):
            nfc = min(4, NPAIR - 4 * j)
            add_unit(f"dn{j}", [(lambda u, nfc=nfc: v3(u, nfc, 1024),
                                 w_dn[4 * j * 128:(4 * j + nfc) * 128, :].rearrange("(fc p) n -> p fc n", p=128))], n=nfc * 1024)
        unit_ap = {name: u for name, u, _ in units}
        def convert(names, after=()):
            for name, u, srcs in units:
                if name in names:
                    for f, s_ in srcs:
                        p.dma("pool", f(u), s_, key="cv_" + name, after=after)

        def last_pe():
            return [p.ops["pe"][-1]] if p.ops["pe"] else []

        converted = set()
        LA = 7

        def load_unit(name):
            def ld(slot, key):
                n = unit_len[name]
                p.dma("sp", slot[:, 0:n], unit_ap[name][:, 0:n], key=key)
            return ld

        def load_unit_direct(name):
            def ld(slot, key):
                srcs_ = unit_srcs[name]
                for i_, (f, s_) in enumerate(srcs_):
                    p.dma("pool", f(slot), s_, key=key + "s", last=(i_ == len(srcs_) - 1))
                n = unit_len[name]
                p.dma("sp", unit_ap[name][:, 0:n], slot[:, 0:n], key="cv_" + name)
            return ld

        def load_ada(v, part):
            def ld(slot, key):
                c0 = v * 1024 + part * 512
                p.dma("pool", slot[:, :].rearrange("p (kc n) -> p kc n", kc=KC, n=512),
                      ada_w[:, c0:c0 + 512].rearrange("(kc p) n -> p kc n", p=128), key=key + "s")
            return ld

        class WStream:
            def __init__(self, seq):
                self.seq = seq
                self.issued = 0
                self.cur = 0

            def acquire(self, name):
                assert self.seq[self.cur][0] == name, (self.seq[self.cur][0], name)
                while self.issued < min(len(self.seq), self.cur + NSLOT - 1):
                    self.seq[self.issued][1](wring[self.issued % NSLOT], f"wr{self.issued % NSLOT}")
                    self.issued += 1
                slot = wring[self.cur % NSLOT]
                self.cur += 1
                return slot

        def ada_units(v):
            return [(f"ada{v}_{part}", load_ada(v, part)) for part in range(2)]

        base = ["q", "kvg", "dq", "dk", "dv", "z", "wa0", "ga0", "wa1", "ga1", "wd0", "gd0", "wd1", "gd1"]
        tail_o = ["wo0", "wo1"]
        ups = [f"up{j}" for j in range(11)]
        dns = [f"dn{j}" for j in range(6)]
        U_ = lambda names: [(n, load_unit(n)) for n in names]
        UD_ = lambda names: [(n, load_unit_direct(n)) for n in names]
        seq0 = ada_units(0) + ada_units(1) + UD_(base) + ada_units(2) + UD_(tail_o) + ada_units(3) + ada_units(4) + UD_(ups) + ada_units(5) + UD_(dns)
        seqn = U_(base + tail_o + ups + dns)
        W = WStream(seq0 + [e for _ in range(ntiles - 1) for e in seqn])

        p.dma("sp", cst, consts, key="c0")
        p.dma("sp", vec, vecs, key="c1")
        p.dma("sp", rowb, bass.AP(rowc.tensor, 0, [[0, 128], [1, 16]]), key="c2")
        p.dma("sp", cact, cT.rearrange("(kc p) s -> p kc s", p=128), key="c7")
        p.dma("sp", xh[0], x[0, 0:T, :].rearrange("(b p) d -> p b d", p=128), key="x0")
        cp("dve", identb, ident)
        memset("dve", onesb, 1.0)
        memset("dve", vaug[:, :, :, 64:66], 1.0)
        act(cact, cact, AF.Silu)
        act(negexpalog, rowb[:, 0:4], AF.Exp)
        ts("dve", negexpalog, negexpalog, -1.0, ALU.mult)
        act(expsink, rowb[:, 8:16], AF.Exp)

        modv = [alloc([128]) for _ in range(6)]
        r0 = alloc([4096])

        def modcol(v):
            return modv[v][:, 0:KC * NSEQ].rearrange("p (j s) -> p j s", s=NSEQ)

        cactb = alloc([KC, NSEQ], BF16)
        cp("dve", cactb, cact)

        def ada_vec(v):
            pT = psum()
            rows = r0[:, 0:512]
            for part in range(2):
                sl = v3(W.acquire(f"ada{v}_{part}"), KC, 512)
                pr = psum()
                for kc in range(KC):
                    mm(pr[0:NSEQ, :], cactb[:, kc, :], sl[:, kc, :], start=(kc == 0), stop=(kc == KC - 1))
                cp("dve", rows[0:NSEQ, :], pr[0:NSEQ, :])
                for j4 in range(4):
                    j = part * 4 + j4
                    tr(pT[:, NSEQ * j:NSEQ * (j + 1)], rows[0:NSEQ, j4 * 128:(j4 + 1) * 128], ident[0:NSEQ, 0:NSEQ])
            tt("dve", modcol(v), pT[:, 0:KC * NSEQ].rearrange("p (j s) -> p j s", s=NSEQ),
               vec[:, V_ADAB + 8 * v:V_ADAB + 8 * v + 8].unsqueeze(2).to_broadcast([128, KC, NSEQ]), ALU.add)

        ada_vec(0)
        ada_vec(1)

        ov_save = state["off"]
        raug = alloc([8])
        selsb = alloc([512])
        Esb = alloc([512])
        Hk = alloc([8, 128])
        lm_st = alloc([7 * 128])
        p.dma("sp", lm_st, lmask, key="c8")
        cp("dve", mk, lm_st[:, :].rearrange("p (l j) -> p l j", l=7))
        memset("dve", raug[32:33, :], -BIG)
        p.dma("sp", raug[0:32, :], relb, key="c3")
        p.dma("sp", selsb[0:33, :], selc, key="c4")
        pE = psum()
        mm(pE[0:8, :], raug[0:33, 0:8], selsb[0:33, :])
        cp("dve", Esb[0:8, :], pE[0:8, :])
        p.dma("sp", escr.rearrange("t h j -> h t j"), Esb[0:8, :].rearrange("h (t j) -> h t j", t=2), key="c5")
        for t_ in range(2):
            p.dma("sp", Hk, bass.AP(escr.tensor, t_ * 8 * 256, [[1, 128], [256, 8], [1, 128]]), key="c6")
            for hh in range(2):
                pb_ = psum()
                for h in range(4):
                    mm(pb_[:, h * 128:(h + 1) * 128], Hk[:, hh * 4 + h, :], Jm)
                cp("dve", biasT[t_][:, hh * 4:(hh + 1) * 4, :], pb_[:, :].rearrange("p (h q) -> p h q", h=4))
        state["off"] = ov_save

        OV = state["off"]
        xn = [r0[:, b * 1024:(b + 1) * 1024] for b in range(NB)]
        DK_ = 2
        dF = [[r0[:, (2 * j + i) * 512:(2 * j + i + 1) * 512].rearrange("p (h i) -> p h i", h=4) for i in range(2)] for j in range(DK_)]
        m1 = r0[:, 2048:4096].bitcast(BF16).rearrange("p (c t) -> p c t", c=KC)
        junkF = r0[:, 0:256].bitcast(BF16)
        OV1 = state["off"]
        qT = alloc([4, T], BF16)
        yaT = alloc([4, T], BF16)
        dqk = alloc([KC, T], BF16)
        dqT = dqk[:, 0:4, :]
        dkT = dqk[:, 4:8, :]
        dvT = alloc([4, T], BF16)
        szT = alloc([4, T], BF16)
        ydT = alloc([4, T], BF16)
        mT = dqk
        gcol = alloc([NB, 4])
        bcol = alloc([NB, 4])
        gtmp = alloc([NB, 4])
        OV2 = state["off"]
        ends = []
        BK_ = 9
        Bset = [dict(acc=alloc([T]), sl=alloc([T]), sqb=alloc([T], BF16)) for _ in range(BK_)]
        ends.append(state["off"]); state["off"] = OV2
        sc_r = Ring([512], F32, 2)
        pT_r = Ring([4, 128], BF16, 4)
        yat_r = Ring([512], BF16, 2)
        sg_r = Ring([512], F32, 2)
        DN_NAMES = ["ktok", "vtok", "qd", "AqkT", "A", "Ua", "Ub", "Va", "Vb", "Yp", "Al0", "Al1", "TbT", "kw", "ktl", "nwT"]
        Dset = []
        for j in range(DK_):
            d_ = {n: alloc([4, 128], BF16) for n in DN_NAMES}
            d_["sm"] = alloc([64])
            d_["F0"], d_["F1"] = dF[j]
            Dset.append(d_)
        ends.append(state["off"]); state["off"] = OV2
        ytmp_r = Ring([D], F32, 2)
        ends.append(state["off"])
        OV_MIX_END = max(ends)
        state["off"] = OV1
        HK_ = 4
        Hset = [dict(a0=alloc([T]), a1=alloc([T]), tmp=alloc([T])) for _ in range(HK_)]
        hidT = alloc([NPAIR, T], BF16)
        ytmp2_r = Ring([D], F32, 2)
        state["off"] = max(state["off"], OV_MIX_END)
        print("SBUF arena bytes used:", state["off"], "of", ARENA_E * 2)

        def VC(base, c):
            return vec[:, base + c:base + c + 1]

        def run_tasks(tasks, admit_n=None):
            pending = list(tasks)
            active = []
            counters = {}
            admit_n = admit_n or {}
            while pending or active:
                admitted = {}
                rest = []
                for (cls, k, fn) in pending:
                    nact = sum(1 for a in active if a[0] == cls)
                    if admitted.get(cls, 0) < admit_n.get(cls, 1) and nact < k and admitted.get(cls, 0) >= 0:
                        c = counters.get(cls, 0)
                        counters[cls] = c + 1
                        active.append((cls, fn(c % k)))
                        admitted[cls] = admitted.get(cls, 0) + 1
                    else:
                        admitted[cls] = -1
                        rest.append((cls, k, fn))
                pending = rest
                still = []
                for cls, g in active:
                    try:
                        next(g)
                        still.append((cls, g))
                    except StopIteration:
                        pass
                active = still

        def rstd_from_ss(ss_ap, n, inv_n):
            r = small.next()[:, 0:n]
            ts("dve", r, ss_ap, inv_n, ALU.mult, RMS_EPS, ALU.add)
            act(r, r, AF.Sqrt)
            p.I("dve", "reciprocal", out=r, in_=r)
            return r

        def prenorm_part1(xt):
            ss = small.next()[:, 0:NB]
            for b in range(NB):
                act(xn[b][:, 0:512].bitcast(BF16), xt[:, b, :], AF.Square, accum_out=ss[:, b:b + 1])
            r = rstd_from_ss(ss, NB, 1.0 / D)
            for b in range(NB):
                if b % 2 == 0:
                    act(xn[b], xt[:, b, :], AF.Copy, scale=r[:, b:b + 1])
                else:
                    ts("dve", xn[b], xt[:, b, :], r[:, b:b + 1], ALU.mult)

        def prenorm_part2(wcol, shv, s):
            for kc in range(KC):
                ps = psum()
                for b in range(NB):
                    tr(ps[:, b * 128:(b + 1) * 128], xn[b][:, kc * 128:(kc + 1) * 128], ident)
                if kc % 2 == 0:
                    act(uT[:, kc, :], ps[:, :], AF.Identity, bias=modcol(shv)[:, kc, s:s + 1], scale=wcol[:, kc:kc + 1])
                else:
                    ts("dve", uT[:, kc, :], ps[:, :], wcol[:, kc:kc + 1], ALU.mult, modcol(shv)[:, kc, s:s + 1], ALU.add)

        def prenorm_to_uT(xt, wcol, shv, s):
            prenorm_part1(xt)
            prenorm_part2(wcol, shv, s)

        def postnorm_residual(xt, b, phs, gw, ytr, jk=None):
            jk = junkF if jk is None else jk
            ss2 = small.next()[:, 0:2]
            for hf in range(2):
                act(jk, phs[hf][:, :], AF.Square, accum_out=ss2[:, hf:hf + 1])
            ss = small.next()[:, 0:1]
            tt("dve", ss, ss2[:, 0:1], ss2[:, 1:2], ALU.add)
            r = rstd_from_ss(ss, 1, 1.0 / D)
            yt = ytr.next()
            for hf in range(2):
                stt(yt[:, hf * 512:(hf + 1) * 512], phs[hf][:, :], r[:, 0:1], gw[:, hf * 512:(hf + 1) * 512], ALU.mult, ALU.mult)
            tt("pool", xt[:, b, :], xt[:, b, :], yt, ALU.add)

        def postnorm_batch(xt, bankpairs, gw, ytr, jk):
            ss2 = small.next()[:, 0:2 * NB]
            for b in range(NB):
                for hf in range(2):
                    act(jk, bankpairs[b][hf][:, :], AF.Square, accum_out=ss2[:, 2 * b + hf:2 * b + hf + 1])
            ss2v = ss2.rearrange("p (b h) -> p b h", h=2)
            ss = small.next()[:, 0:NB]
            tt("dve", ss, ss2v[:, :, 0], ss2v[:, :, 1], ALU.add)
            r = rstd_from_ss(ss, NB, 1.0 / D)
            for b in range(NB):
                yt = ytr.next()
                for hf in range(2):
                    stt(yt[:, hf * 512:(hf + 1) * 512], bankpairs[b][hf][:, :], r[:, b:b + 1], gw[:, hf * 512:(hf + 1) * 512], ALU.mult, ALU.mult)
                tt("pool", xt[:, b, :], xt[:, b, :], yt, ALU.add)

        def setup_wcol(wdst, scv, vbase, s):
            stt(wdst, modcol(scv)[:, :, s], 1.0, vec[:, vbase:vbase + 8], ALU.add, ALU.mult)

        def setup_gw(gw, gv, vbase, s):
            gc_ = small.next()[:, 0:8]
            tt("dve", gc_, modcol(gv)[:, :, s], vec[:, vbase:vbase + 8], ALU.mult)
            for hf in range(2):
                ps = psum()
                for k4 in range(4):
                    kc = hf * 4 + k4
                    cb = xn[k4][:, 0:128]
                    cp("dve", cb, gc_[:, kc:kc + 1].to_broadcast([128, 128]))
                    mm(ps[:, k4 * 128:(k4 + 1) * 128], cb, ident)
                cp("act", gw[:, hf * 512:(hf + 1) * 512], ps[:, :])

        def stage(name):
            if stop == name:
                raise _Stop()

        tile_no = 0
        hoisted = {"done": False}
        try:
          stage('P')
          for s in range(NSEQ):
            if tile_no >= ntiles:
                break
            memset("pool", Sst, 0.0)
            memset("pool", Sbf[0], 0.0)
            memset("pool", halo_dn[tile_no % 2], 0.0)
            memset("pool", halo_ffn[tile_no % 2], 0.0)
            sbc = {"i": 0}
            for ti in range(NT):
                if tile_no >= ntiles:
                    break
                first = (ti == 0)
                xt = xh[tile_no % 2]
                state["nrot"] = 8
                nt_ = tile_no + 1
                if nt_ < ntiles:
                    ns, nti = divmod(nt_, NT)
                    p.dma("sp", xh[nt_ % 2], x[ns, nti * T:(nti + 1) * T, :].rearrange("(b p) d -> p b d", p=128), key=f"x{nt_ % 2}")
                if first:
                    setup_wcol(w1s, 1, V_NMP, s)
                if not hoisted["done"]:
                    prenorm_part1(xt)
                hoisted["done"] = False
                prenorm_part2(w1s, 0, s)
                if "uT" in dbg and tile_no == 0:
                    dbg_dump("uT", uT, [128, KC, T])
                stage('A')
                sl = W.acquire("q")
                wq = v3(sl, 8, 512)
                for c in range(4):
                    ps = psum()
                    for kc in range(KC):
                        mm(ps[:, :], wq[:, kc, c * 128:(c + 1) * 128], uT[:, kc, :], start=(kc == 0), stop=(kc == KC - 1))
                    act(qT[:, c, :], ps[:, :], AF.Copy, scale=0.125)
                sl = W.acquire("kvg")
                wk = v3(sl, 8, 264)
                ps = psum()
                for kc in range(KC):
                    mm(ps[:, :], wk[:, kc, 0:128], uT[:, kc, :], start=(kc == 0), stop=(kc == KC - 1))
                cp("act", kT[:, 128:128 + T], ps[:, :])
                ps = psum()
                for b in range(NB):
                    for kc in range(KC):
                        mm(ps[:, b * 128:(b + 1) * 128], uT[:, kc, b * 128:(b + 1) * 128], wk[:, kc, 128:256], start=(kc == 0), stop=(kc == KC - 1))
                cp("dve", vaug[:, 1:NB + 1, :, 0:64], ps[:, :].rearrange("p (b k d) -> p b k d", b=NB, k=2))
                ps = psum()
                for b in range(NB):
                    for kc in range(KC):
                        mm(ps[:, b * 8:(b + 1) * 8], uT[:, kc, b * 128:(b + 1) * 128], wk[:, kc, 256:264], start=(kc == 0), stop=(kc == KC - 1))
                pg3 = ps[:, 0:NB * 8].rearrange("p (b c) -> p b c", c=8)
                act(bcol, pg3[:, :, 0:4], AF.Sigmoid)
                tt("dve", gtmp, pg3[:, :, 4:8], rowb[:, 4:8].unsqueeze(1).to_broadcast([128, NB, 4]), ALU.add)
                act(gtmp, gtmp, AF.Exp)
                act(gtmp, gtmp, AF.Ln, bias=1.0)
                tt("dve", gcol, gtmp, negexpalog[:, 0:4].unsqueeze(1).to_broadcast([128, NB, 4]), ALU.mult)
                wunit = {}

                def conv_task(ui, hh, dst):
                    def gen(j):
                        B_ = Bset[j]
                        ch = ui * 4 + hh
                        if hh == 0:
                            wunit[ui] = v3(W.acquire(("dq", "dk", "dv")[ui]), 8, 512)
                        wu = wunit[ui]
                        ps = psum()
                        for kc in range(KC):
                            mm(ps[:, :], wu[:, kc, hh * 128:(hh + 1) * 128], uT[:, kc, :], start=(kc == 0), stop=(kc == KC - 1))
                        acc = B_["acc"]
                        hold = halo_dn[tile_no % 2][:, ch, :]
                        act(acc, ps[:, :], AF.Copy, scale=VC(V_DNC, 3 * 12 + ch))
                        cp("act", halo_dn[(tile_no + 1) % 2][:, ch, :], ps[:, T - 3:T])
                        for jt in range(3):
                            sh = 3 - jt
                            wj = VC(V_DNC, jt * 12 + ch)
                            stt(acc[:, sh:T], ps[:, 0:T - sh], wj, acc[:, sh:T], ALU.mult, ALU.add)
                            stt(acc[:, 0:sh], hold[:, 3 - sh:3], wj, acc[:, 0:sh], ALU.mult, ALU.add)
                        yield
                        yield
                        if ui == 2:
                            act(dst[:, hh, :], acc, AF.Silu)
                            return
                        slv, sqb = B_["sl"], B_["sqb"]
                        act(slv, acc, AF.Silu)
                        tt("pool", sqb, slv, slv, ALU.mult)
                        yield
                        ps2 = psum()
                        mm(ps2[:, :], onesb, sqb)
                        act(acc, ps2[:, :], AF.Ln, bias=L2_EPS)
                        act(acc, acc, AF.Exp, scale=-0.5, bias=(-0.5 * float(np.log(128.0)) if ui == 0 else 0.0))
                        tt("dve", dst[:, hh, :], slv, acc, ALU.mult)
                    return gen

                def attn_task(j):
                    for b in range(NB):
                        gb = ti * NB + b
                        kbs = ([0] if gb > 0 else []) + [1]
                        po = [psA[0], psA[1]]
                        for kv in range(2):
                            pts = {}
                            for kb in kbs:
                                ps = psum()
                                kcol = (b + kb) * 128
                                mm(ps[:, :], kT[64 * kv:64 * kv + 64, kcol:kcol + 128], qT[64 * kv:64 * kv + 64, :, b * 128:(b + 1) * 128])
                                sc = sc_r.next()
                                tt("dve", sc, ps[:, :], biasT[kb][:, 4 * kv:4 * kv + 4, :], ALU.add)
                                pt = pT_r.next()
                                act(pt, sc, AF.Exp)
                                pts[kb] = pt
                            for g in range(4):
                                for n_, kb in enumerate(kbs):
                                    mm(po[kv][:, g * 65:(g + 1) * 65], pts[kb][:, g, :], vaug[:, b + kb, kv, 0:65],
                                       start=(n_ == 0), stop=(n_ == len(kbs) - 1))
                            yield
                        yat = yat_r.next()
                        for kv in range(2):
                            pov = po[kv][:, 0:260].rearrange("p (g d) -> p g d", d=65)
                            den = small.next()[:, 0:4]
                            tt("dve", den, pov[:, :, 64], expsink[:, 4 * kv:4 * kv + 4], ALU.add)
                            p.I("dve", "reciprocal", out=den, in_=den)
                            tt("dve", yat[:, kv * 256:(kv + 1) * 256].rearrange("p (g d) -> p g d", d=64), pov[:, :, 0:64],
                               den.unsqueeze(2).to_broadcast([128, 4, 64]), ALU.mult)
                        yield
                        psb = psum()[:, :].bitcast(BF16)
                        for c in range(4):
                            tr(psb[:, c * 128:(c + 1) * 128], yat[:, c * 128:(c + 1) * 128], identb)
                        cp("act", yaT[:, :, b * 128:(b + 1) * 128], psb[:, 0:512].rearrange("p (c t) -> p c t", c=4))
                        yield
                    cp("pool", kT[:, 0:128], kT[:, T:T + 128])
                    cp("pool", vaug[:, 0, :, 0:64], vaug[:, NB, :, 0:64])

                def dn_task(b):
                    def gen(j):
                        S_ = Dset[j]
                        blk = slice(b * 128, (b + 1) * 128)
                        F0, F1 = S_["F0"], S_["F1"]
                        ktok, vtok = S_["ktok"], S_["vtok"]
                        sm = S_["sm"]
                        gc_c, ngc_c, egc, tail, glb, nbeta = [sm[:, 4 * i:4 * i + 4] for i in range(6)]
                        r4 = lambda t_: t_[:, :].rearrange("p (h i) -> p h i", h=4)
                        for (srcT, dstk, eng) in ((dkT, ktok, "act"), (dvT, vtok, "dve")):
                            psb = psum()[:, :].bitcast(BF16)
                            for h in range(4):
                                tr(psb[:, h * 128:(h + 1) * 128], srcT[:, h, blk], identb)
                            cp(eng, dstk, psb[:, 0:512].rearrange("p (h d) -> p h d", h=4))
                        pgc = psum()
                        mm(pgc[:, 0:4], Umat, gcol[:, b, :])
                        cp("dve", gc_c, pgc[:, 0:4])
                        ts("dve", ngc_c, gc_c, -1.0, ALU.mult)
                        cp("dve", F0, gcol[:, b, :].unsqueeze(2).to_broadcast([128, 4, 128]))
                        ts("dve", nbeta, bcol[:, b, :], -1.0, ALU.mult)
                        act(egc, gc_c, AF.Exp)
                        yield
                        pgcb = psum()
                        for h in range(4):
                            mm(pgcb[:, h * 128:(h + 1) * 128], F0[:, h, :], Umat)
                        pgcb3 = r4(pgcb)
                        act(F1, pgcb3, AF.Exp)
                        tt("dve", tail, pgcb3[:, :, 127], gc_c, ALU.subtract)
                        act(glb, pgcb3[:, :, 127], AF.Exp)
                        tt("dve", F0, pgcb3, M1S.unsqueeze(1).to_broadcast([128, 4, 128]), ALU.add)
                        tt("dve", S_["qd"], dqT[:, :, blk], F1, ALU.mult)
                        tt("dve", F1, pgcb3, M2.unsqueeze(1).to_broadcast([128, 4, 128]), ALU.add)
                        act(tail, tail, AF.Exp)
                        yield
                        for h in range(4):
                            act(F0[:, h, :], F0[:, h, :], AF.Exp, scale=-1.0, bias=gc_c[:, h:h + 1])
                            act(F1[:, h, :], F1[:, h, :], AF.Exp, scale=1.0, bias=ngc_c[:, h:h + 1])
                        pkk = psum()
                        for h in range(4):
                            mm(pkk[:, h * 128:(h + 1) * 128], dkT[:, h, blk], dkT[:, h, blk])
                        pqk = psum()
                        for h in range(4):
                            mm(pqk[:, h * 128:(h + 1) * 128], dkT[:, h, blk], dqT[:, h, blk])
                        A = S_["A"]
                        for h in range(4):
                            stt(A[:, h, :], pkk[:, h * 128:(h + 1) * 128], nbeta[:, h:h + 1], F0[:, h, :], ALU.mult, ALU.mult)
                        tt("dve", S_["AqkT"], r4(pqk), F1, ALU.mult)
                        tt("pool", S_["kw"], ktok, egc.unsqueeze(2).to_broadcast([128, 4, 128]), ALU.mult)
                        tt("pool", S_["ktl"], ktok, tail.unsqueeze(2).to_broadcast([128, 4, 128]), ALU.mult)
                        yield
                        mkb = lambda l: mk[:, l, :].unsqueeze(1).to_broadcast([128, 4, 128])
                        Al = [S_["Al0"], S_["Al1"]]
                        tt("pool", Al[0], A, mkb(0), ALU.mult)
                        yield
                        pat = psum()[:, :].bitcast(BF16)
                        for h in range(4):
                            tr(pat[:, h * 128:(h + 1) * 128], Al[0][:, h, :], identb)
                        pat3 = pat[:, 0:512].rearrange("p (h i) -> p h i", h=4)
                        Ub_ = [S_["Ua"], S_["Ub"]]
                        Vb_ = [S_["Va"], S_["Vb"]]
                        U, V = Ub_[0], Vb_[0]
                        tt("dve", V, pat3, identb.unsqueeze(1).to_broadcast([128, 4, 128]), ALU.add)
                        tt("pool", U, Al[0], identb.unsqueeze(1).to_broadcast([128, 4, 128]), ALU.add)
                        tt("pool", Al[1], A, mkb(1), ALU.mult)
                        yield
                        Yp = S_["Yp"]
                        for lvl in range(1, 7):
                            Acur = Al[lvl % 2]
                            pY = psum()
                            for h in range(4):
                                mm(pY[:, h * 128:(h + 1) * 128], Acur[:, h, :], V[:, h, :])
                            cp("act", Yp, r4(pY))
                            if lvl < 6:
                                tt("pool", Al[(lvl + 1) % 2], A, mkb(lvl + 1), ALU.mult)
                            yield
                            pu = psum()
                            for h in range(4):
                                mm(pu[:, h * 128:(h + 1) * 128], Yp[:, h, :], U[:, h, :])
                            pv = psum()
                            for h in range(4):
                                mm(pv[:, h * 128:(h + 1) * 128], U[:, h, :], Yp[:, h, :])
                            Un, Vn = Ub_[lvl % 2], Vb_[lvl % 2]
                            tt("dve", Un, r4(pu), U, ALU.add)
                            tt("dve", Vn, r4(pv), V, ALU.add)
                            U, V = Un, Vn
                            yield
                        X = V
                        TbT = S_["TbT"]
                        tt("dve", TbT, X, bcol[:, b, :].unsqueeze(2).to_broadcast([128, 4, 128]), ALU.mult)
                        yield
                        pw = psum()
                        for h in range(4):
                            mm(pw[:, h * 128:(h + 1) * 128], S_["kw"][:, h, :], TbT[:, h, :])
                        nwT = S_["nwT"]
                        act(nwT, r4(pw), AF.Copy, scale=-1.0)
                        yield
                        Sb = Sbf[sbc["i"] % 2]
                        pv = psum()
                        for h in range(4):
                            mm(pv[:, h * 128:(h + 1) * 128], TbT[:, h, :], vtok[:, h, :], start=True, stop=False)
                            mm(pv[:, h * 128:(h + 1) * 128], nwT[:, h, :], Sb[:, h, :], start=False, stop=True)
                        vnew = S_["kw"]
                        cp("act", vnew, pv[:, :].rearrange("p (h e) -> p h e", h=4))
                        yield
                        po_ = psum()
                        for h in range(4):
                            mm(po_[:, h * 128:(h + 1) * 128], Sb[:, h, :], S_["qd"][:, h, :], start=True, stop=False)
                            mm(po_[:, h * 128:(h + 1) * 128], vnew[:, h, :], S_["AqkT"][:, h, :], start=False, stop=True)
                        psu = psum()
                        for h in range(4):
                            mm(psu[:, h * 128:(h + 1) * 128], S_["ktl"][:, h, :], vnew[:, h, :])
                        for h in range(4):
                            stt(Sst[:, h, :], Sst[:, h, :], glb[:, h:h + 1], psu[:, h * 128:(h + 1) * 128], ALU.mult, ALU.add)
                        sbc["i"] += 1
                        cp("act", Sbf[sbc["i"] % 2], Sst)
                        sq = S_["A"]
                        act(sq, r4(po_), AF.Square)
                        cp("dve", F1, r4(po_))
                        yield
                        pss = psum()
                        mm(pss[:, :], onesb, sq[:, :, :].rearrange("p h i -> p (h i)"))
                        act(F0, r4(pss), AF.Ln, scale=1.0 / 128.0, bias=RMS_EPS)
                        yield
                        act(F0, F0, AF.Exp, scale=-0.5)
                        if "dn_o" in dbg and tile_no == 0 and b <= 1:
                            dbg_dump("dn_o", F1, [128, 4, 128])
                        tt("dve", F1, F1, F0, ALU.mult)
                        stt(ydT[:, :, blk], F1, VC(V_DNW, 0), szT[:, :, blk], ALU.mult, ALU.mult)
                    return gen

                def mergeA_task(j):
                    for hf in range(2):
                        wb_ = v3(W.acquire(f"wa{hf}"), 4, 512)
                        wg = v3(W.acquire(f"ga{hf}"), 8, 512)
                        for cc in range(4):
                            c = hf * 4 + cc
                            pg_ = psum()
                            for kc in range(KC):
                                mm(pg_[:, :], wg[:, kc, cc * 128:(cc + 1) * 128], uT[:, kc, :], start=(kc == 0), stop=(kc == KC - 1))
                            sg = sg_r.next()
                            act(sg, pg_[:, :], AF.Sigmoid)
                            yield
                            pb_ = psum()
                            for kc in range(4):
                                mm(pb_[:, :], wb_[:, kc, cc * 128:(cc + 1) * 128], yaT[:, kc, :], start=(kc == 0), stop=(kc == 3))
                            tt("dve", m1[:, c, :], pb_[:, :], sg, ALU.mult)
                            yield

                run_tasks([("B", BK_, conv_task(ui, hh, dst)) for ui, dst in enumerate((dqT, dkT, dvT)) for hh in range(4)], admit_n={"B": 2})
                sl = W.acquire("z")
                wu = v3(sl, 8, 512)
                for hh in range(4):
                    ps = psum()
                    for kc in range(KC):
                        mm(ps[:, :], wu[:, kc, hh * 128:(hh + 1) * 128], uT[:, kc, :], start=(kc == 0), stop=(kc == KC - 1))
                    act(szT[:, hh, :], ps[:, :], AF.Silu)
                stage('B')
                def attn_merge_task(j):
                    yield from attn_task(j)
                    yield
                    yield from mergeA_task(j)

                state["nrot"] = NROT
                run_tasks([("D", DK_, dn_task(b)) for b in range(NB)] + [("C", 1, attn_merge_task)])
                state["nrot"] = 8
                if "yaT" in dbg and tile_no <= 1:
                    dbg_dump("yaT", yaT, [128, 4, T])
                if "ydT" in dbg and tile_no <= 1:
                    dbg_dump("ydT", ydT, [128, 4, T])
                stage('D')
                for hf in range(2):
                    wb_ = v3(W.acquire(f"wd{hf}"), 4, 512)
                    wg = v3(W.acquire(f"gd{hf}"), 8, 512)
                    for cc in range(4):
                        c = hf * 4 + cc
                        pb_ = psum()
                        for kc in range(4):
                            mm(pb_[:, :], wb_[:, kc, cc * 128:(cc + 1) * 128], ydT[:, kc, :], start=(kc == 0), stop=(kc == 3))
                        pg_ = psum()
                        for kc in range(KC):
                            mm(pg_[:, :], wg[:, kc, cc * 128:(cc + 1) * 128], uT[:, kc, :], start=(kc == 0), stop=(kc == KC - 1))
                        sg = sg_r.next()
                        act(sg, pg_[:, :], AF.Sigmoid)
                        tt("dve", sg, pb_[:, :], sg, ALU.mult)
                        tt("pool", mT[:, c, :], sg, m1[:, c, :], ALU.add)
                if "mT" in dbg and tile_no == 0:
                    dbg_dump("mT", mT, [128, KC, T])
                stage('E')
                if first:
                    if tile_no == 0:
                        ada_vec(2)
                    setup_gw(g1w, 2, V_NMPOST, s)
                wo = [v3(W.acquire("wo0"), 8, 512), v3(W.acquire("wo1"), 8, 512)]
                pend = None
                for b in range(NB):
                    phs = [psum(), psum()]
                    for hf in range(2):
                        for kc in range(KC):
                            mm(phs[hf][:, :], mT[:, kc, b * 128:(b + 1) * 128], wo[hf][:, kc, :], start=(kc == 0), stop=(kc == KC - 1))
                    if pend is not None:
                        postnorm_residual(xt, pend[0], pend[1], g1w, ytmp_r)
                    pend = (b, phs)
                postnorm_residual(xt, pend[0], pend[1], g1w, ytmp_r)
                if "h1" in dbg and tile_no == 0:
                    dbg_dump("h1", xt, [128, NB, D])
                stage('F')
                if first:
                    if tile_no == 0:
                        ada_vec(3)
                        ada_vec(4)
                    setup_wcol(w2s, 4, V_NFP, s)
                prenorm_to_uT(xt, w2s, 3, s)
                stage('G')
                wunit = {}

                def ffn_task(pair):
                    def gen(j):
                        H_ = Hset[j]
                        jj, pp_ = divmod(pair, 2)
                        if pp_ == 0:
                            wunit[jj] = W.acquire(f"up{jj}")[:, :].rearrange("p (kc two n) -> p kc two n", kc=8, two=2, n=256)
                        wu = wunit[jj]
                        for two in range(2):
                            ps = psum()
                            for kc in range(KC):
                                mm(ps[:, :], wu[:, kc, two, pp_ * 128:(pp_ + 1) * 128], uT[:, kc, :], start=(kc == 0), stop=(kc == KC - 1))
                            ch = two * NPAIR + pair
                            acc = H_[f"a{two}"]
                            hold = halo_ffn[tile_no % 2][:, ch, :]
                            wc = lambda jt: VC(V_FFC, jt * 2 * NPAIR + ch)
                            act(acc, ps[:, :], AF.Copy, scale=wc(2))
                            cp("act", halo_ffn[(tile_no + 1) % 2][:, ch, :], ps[:, T - 2:T])
                            if two == 1:
                                tmp = H_["tmp"]
                                act(tmp[:, 1:T], ps[:, 0:T - 1], AF.Copy, scale=wc(1))
                                act(tmp[:, 0:1], hold[:, 1:2], AF.Copy, scale=wc(1))
                            for jt in range(2 if two == 0 else 1):
                                sh = 2 - jt
                                stt(acc[:, sh:T], ps[:, 0:T - sh], wc(jt), acc[:, sh:T], ALU.mult, ALU.add)
                                stt(acc[:, 0:sh], hold[:, 2 - sh:2], wc(jt), acc[:, 0:sh], ALU.mult, ALU.add)
                            if two == 1:
                                tt("pool", acc, acc, tmp, ALU.add)
                        yield
                        yield
                        act(H_["a0"], H_["a0"], AF.Gelu_apprx_tanh)
                        tt("pool", hidT[:, pair, :], H_["a0"], H_["a1"], ALU.mult)
                    return gen

                run_tasks([("H", HK_, ffn_task(pair)) for pair in range(NPAIR)])
                stage('H')
                if first:
                    if tile_no == 0:
                        ada_vec(5)
                    setup_gw(g2w, 5, V_NFPOST, s)
                banks = list(psums)
                for j in range(6):
                    nfc = min(4, NPAIR - 4 * j)
                    sl = W.acquire(f"dn{j}")
                    wdn_ = v3(sl, nfc, 1024)
                    for fcl in range(nfc):
                        fc = 4 * j + fcl
                        for b in range(NB):
                            for hf in range(2):
                                mm(banks[b * 2 + hf][:, :], hidT[:, fc, b * 128:(b + 1) * 128], wdn_[:, fcl, hf * 512:(hf + 1) * 512],
                                   start=(fc == 0), stop=(fc == NPAIR - 1))
                if nt_ < ntiles:
                    prenorm_part1(xh[nt_ % 2])
                    hoisted["done"] = True
                jk2 = Hset[0]["tmp"][:, 0:256].bitcast(BF16)
                postnorm_batch(xt, [[banks[b * 2], banks[b * 2 + 1]] for b in range(NB)], g2w, ytmp2_r, jk2)
                p.dma("sp", out[s, ti * T:(ti + 1) * T, :].rearrange("(b p) d -> p b d", p=128), xt, key=f"o{tile_no % 2}", is_output=True)
                tile_no += 1
        except _Stop:
            pass
        p.emit()
    return nc, dbg_outs, p


_CACHE = {}


def _layout_inputs(inputs, core):
    f = lambda a: np.ascontiguousarray(np.asarray(a, dtype=np.float32))
    b0 = core * NSEQ
    vecs = np.zeros((128, NV), np.float32)
    col = lambda v: np.asarray(v, np.float32).reshape(-1, 128).T
    vecs[:, V_ADAB:V_ADAB + 48] = col(inputs["ada_b"][0])
    vecs[:, V_NMP:V_NMP + 8] = col(inputs["norm_mix_pre"][0])
    vecs[:, V_NMPOST:V_NMPOST + 8] = col(inputs["norm_mix_post"][0])
    vecs[:, V_NFP:V_NFP + 8] = col(inputs["norm_ffn_pre"][0])
    vecs[:, V_NFPOST:V_NFPOST + 8] = col(inputs["norm_ffn_post"][0])
    dnc = np.asarray(inputs["dn_conv_w"][0], np.float32)
    for j in range(4):
        vecs[:, V_DNC + j * 12:V_DNC + (j + 1) * 12] = col(dnc[j])
    ffc = np.asarray(inputs["ffn_conv_w"][0], np.float32)
    for j in range(3):
        vecs[:, V_FFC + j * 44:V_FFC + (j + 1) * 44] = col(ffc[j])
    vecs[:, V_DNW] = np.asarray(inputs["dn_norm_w"][0], np.float32)
    rowc = np.concatenate([np.asarray(inputs["dn_a_log"][0], np.float32), np.asarray(inputs["dn_dt_bias"][0], np.float32),
                           np.asarray(inputs["attn_sinks"][0], np.float32)])[None, :]
    return {
        "x": f(inputs["x"][b0:b0 + NSEQ]),
        "cT": f(np.asarray(inputs["c"][b0:b0 + NSEQ]).T),
        "vecs": vecs,
        "rowc": f(rowc),
    }


def kernel(**inputs):
    n = 8
    if "nc" not in _CACHE:
        _CACHE["nc"] = build_program()[0]
        _CACHE["consts"] = _host_consts()
    nc = _CACHE["nc"]
    consts, selc, lmask = _CACHE["consts"]
    f = lambda a: np.ascontiguousarray(np.asarray(a, dtype=np.float32))
    shared = {
        "ada_w": f(inputs["ada_w"][0]), "w_in": f(inputs["w_in"][0]), "w_a": f(inputs["w_attn_branch"][0]),
        "w_d": f(inputs["w_dn_branch"][0]), "w_o": f(inputs["w_out"][0]), "w_up": f(inputs["ffn_w_up"][0]),
        "w_dn": f(inputs["ffn_w_down"][0]), "relb": f(inputs["rel_bias"]), "consts": consts, "selc": selc, "lmask": lmask,
    }
    in_maps = []
    for c in range(n):
        m = dict(shared)
        m.update(_layout_inputs(inputs, c))
        in_maps.append(m)
    res = run_bass_kernel_spmd(nc, in_maps, core_ids=list(range(n)))
    return np.concatenate([np.asarray(r["out"]) for r in res.results], axis=0).astype(np.float32)
```
